# Optimizing a Trainium2 kernel written in Bass

```python
import jax
import jax.numpy as jnp
from jax import lax
import numpy as np


D_MODEL = 1024
BATCH = 4
SEQ = 8192
DEPTH = 2

GRID_W = 64
CTX_LEN = 256
EPS = 1e-6
NA_HEADS = 8
NA_HEAD_DIM = 64
NA_WIDTH = NA_HEADS * NA_HEAD_DIM
NA_WIN_R = 8
NA_WIN_C = 16
NA_QBLOCK_C = 16
NA_KBAND_C = NA_QBLOCK_C + NA_WIN_C
POOL_WINDOWS = (2, 4, 8, 16)
POOL_GROUP = 128
POOL_WIDTH = POOL_GROUP * len(POOL_WINDOWS)
MLA_HEADS = 8
MLA_NOPE = 64
MLA_ROPE = 32
MLA_V = 64
MLA_Q_RANK = 512
MLA_KV_RANK = 256
MLA_QBLOCK = 128
ROPE_BASE = 10000.0
N_BRANCH = 3
BRANCH_W = 512
D_FF = 2816
CONV_W = 3
IN_SIZES = (NA_WIDTH, NA_WIDTH, NA_WIDTH, POOL_WIDTH, MLA_Q_RANK, MLA_KV_RANK, MLA_ROPE, N_BRANCH * D_MODEL)
IN_COLS = 3 * NA_WIDTH + POOL_WIDTH + MLA_Q_RANK + MLA_KV_RANK + MLA_ROPE + N_BRANCH * D_MODEL

kernel_name = 'hybrid_dit_na_pool_mla_block'


def rms_norm(x, g):
    xf = x.astype(jnp.float32)
    y = xf * lax.rsqrt(jnp.mean(xf * xf, axis=-1, keepdims=True) + EPS)
    return (y * g.astype(jnp.float32)).astype(x.dtype)


def modulate(h, shift, scale):
    return h * (1 + scale) + shift


def split_cols(z, sizes):
    outs, o = [], 0
    for s in sizes:
        outs.append(z[..., o:o + s])
        o += s
    return outs


def axial_angles(n_tok, dim):
    n_freq = dim // 4
    inv = ROPE_BASE ** (-jnp.arange(n_freq, dtype=jnp.float32) / n_freq)
    t = jnp.arange(n_tok, dtype=jnp.int32)
    row = (t // GRID_W).astype(jnp.float32)
    col = (t % GRID_W).astype(jnp.float32)
    return row[:, None] * inv[None, :], col[:, None] * inv[None, :]


def _rot_half(x, ang):
    x1, x2 = jnp.split(x, 2, axis=-1)
    cos, sin = jnp.cos(ang), jnp.sin(ang)
    return jnp.concatenate([x1 * cos - x2 * sin, x1 * sin + x2 * cos], axis=-1)


def apply_axial_rope(x, ang_r, ang_c):
    xf = x.astype(jnp.float32)
    xr, xc = jnp.split(xf, 2, axis=-1)
    return jnp.concatenate([_rot_half(xr, ang_r), _rot_half(xc, ang_c)], axis=-1).astype(x.dtype)


def dwconv_centred(u, w, b):
    L = u.shape[1]
    pad = CONV_W // 2
    up = jnp.pad(u, ((0, 0), (pad, pad), (0, 0)))
    y = b
    for j in range(CONV_W):
        y = y + up[:, j:j + L] * w[j]
    return y


def pool_mixer(u, w_grp, scale):
    B, L, _ = u.shape
    uf = u.astype(jnp.float32)
    cs = jnp.concatenate([jnp.zeros((B, 1, POOL_WIDTH), jnp.float32), jnp.cumsum(uf, axis=1)], axis=1)
    t = jnp.arange(L)
    diffs = []
    for gi, win in enumerate(POOL_WINDOWS):
        lo = jnp.clip(t - win // 2, 0, L)
        hi = jnp.clip(t + win // 2, 0, L)
        sl = slice(gi * POOL_GROUP, (gi + 1) * POOL_GROUP)
        csg = cs[..., sl]
        cnt = (hi - lo).astype(jnp.float32)[None, :, None]
        diffs.append((csg[:, hi] - csg[:, lo]) / cnt - uf[..., sl])
    d = jnp.stack(diffs, axis=2)
    y = jnp.einsum('blgi,gio->blgo', d, w_grp.astype(jnp.float32)).reshape(B, L, POOL_WIDTH)
    return (y * scale.astype(jnp.float32)).astype(u.dtype)


def dense_attn(q, k, v):
    s = jnp.einsum('bqhd,bkhd->bhqk', q, k, preferred_element_type=jnp.float32) * (q.shape[-1] ** -0.5)
    p = jax.nn.softmax(s, axis=-1).astype(v.dtype)
    return jnp.einsum('bhqk,bkhd->bqhd', p, v)


def na_latent(q, k, v, k_ctx, v_ctx, rpb):
    B, S, H, dh = q.shape
    rows = S // GRID_W
    kr = min(NA_WIN_R, rows)
    ncb = GRID_W // NA_QBLOCK_C
    scale = dh ** -0.5
    qcol = jnp.arange(GRID_W).reshape(ncb, NA_QBLOCK_C)
    win0 = jnp.clip(qcol - NA_WIN_C // 2, 0, GRID_W - NA_WIN_C)
    band0 = jnp.clip(jnp.arange(ncb) * NA_QBLOCK_C - NA_WIN_C // 2, 0, GRID_W - NA_KBAND_C)
    kcol = band0[:, None] + jnp.arange(NA_KBAND_C)[None, :]
    kc3 = kcol[:, None, :]
    in_win = (kc3 >= win0[..., None]) & (kc3 < win0[..., None] + NA_WIN_C)
    rel_c = jnp.clip(kc3 - qcol[..., None] + NA_WIN_C - 1, 0, 2 * NA_WIN_C - 2)
    kg = k.reshape(B, rows, GRID_W, H, dh)
    vg = v.reshape(B, rows, GRID_W, H, dh)
    qg = jnp.moveaxis(q.reshape(B, rows, ncb, NA_QBLOCK_C, H, dh), 1, 0)
    n_win = kr * NA_KBAND_C

    def one_row(args):
        r, q_r = args
        r0 = jnp.clip(r - kr // 2, 0, rows - kr)
        k_rows = lax.dynamic_slice_in_dim(kg, r0, kr, axis=1)
        v_rows = lax.dynamic_slice_in_dim(vg, r0, kr, axis=1)
        k_nb = k_rows[:, :, kcol]
        v_nb = v_rows[:, :, kcol]
        rel_r = r0 + jnp.arange(kr) - r + NA_WIN_R - 1
        bias = rpb[:, rel_r[None, None, :, None], rel_c[:, :, None, :]]
        s_win = jnp.einsum('bjqhd,bkjchd->bhjqkc', q_r, k_nb, preferred_element_type=jnp.float32) * scale
        s_win = jnp.where(in_win[:, :, None, :], s_win + bias.astype(jnp.float32), -jnp.inf)
        s_ctx = jnp.einsum('bjqhd,bnhd->bhjqn', q_r, k_ctx, preferred_element_type=jnp.float32) * scale
        s = jnp.concatenate([s_win.reshape(B, H, ncb, NA_QBLOCK_C, n_win), s_ctx], axis=-1)
        p = jax.nn.softmax(s, axis=-1).astype(v.dtype)
        p_win = p[..., :n_win].reshape(B, H, ncb, NA_QBLOCK_C, kr, NA_KBAND_C)
        p_ctx = p[..., n_win:]
        return (jnp.einsum('bhjqkc,bkjchd->bjqhd', p_win, v_nb)
                + jnp.einsum('bhjqn,bnhd->bjqhd', p_ctx, v_ctx))

    out = lax.map(one_row, (jnp.arange(rows), qg))
    return jnp.moveaxis(out, 0, 1).reshape(B, S, H, dh)


def mla_attend(q_nope, q_rope, k_nope, k_rope, v):
    B, L, H, _ = q_nope.shape
    nb = L // MLA_QBLOCK
    scale = (MLA_NOPE + MLA_ROPE) ** -0.5

    def blk(args):
        qn, qr = args
        s = (jnp.einsum('bqhd,bthd->bhqt', qn, k_nope, preferred_element_type=jnp.float32)
             + jnp.einsum('bqhd,btd->bhqt', qr, k_rope, preferred_element_type=jnp.float32)) * scale
        p = jax.nn.softmax(s, axis=-1).astype(v.dtype)
        return jnp.einsum('bhqt,bthd->bqhd', p, v)

    def to_blocks(a):
        return jnp.moveaxis(a.reshape(B, nb, MLA_QBLOCK, *a.shape[2:]), 1, 0)

    out = lax.map(blk, (to_blocks(q_nope), to_blocks(q_rope)))
    return jnp.moveaxis(out, 0, 1).reshape(B, L, H, MLA_V)


def project_stream(h, lp, rope_angles):
    B, L, _ = h.shape
    q, k, v, u, cq, ckv, kr, g = split_cols(h @ lp['w_in'], IN_SIZES)
    qm = (rms_norm(cq, lp['mla_q_norm']) @ lp['w_uq']).reshape(B, L, MLA_HEADS, MLA_NOPE + MLA_ROPE)
    kvm = (rms_norm(ckv, lp['mla_kv_norm']) @ lp['w_ukv']).reshape(B, L, MLA_HEADS, MLA_NOPE + MLA_V)
    q_rope = qm[..., MLA_NOPE:]
    if rope_angles is not None:
        ang_r, ang_c = rope_angles
        q_rope = apply_axial_rope(q_rope, ang_r[:, None, :], ang_c[:, None, :])
        kr = apply_axial_rope(kr, ang_r, ang_c)
    return {
        'na_q': q.reshape(B, L, NA_HEADS, NA_HEAD_DIM),
        'na_k': k.reshape(B, L, NA_HEADS, NA_HEAD_DIM),
        'na_v': v.reshape(B, L, NA_HEADS, NA_HEAD_DIM),
        'pool_u': u,
        'q_nope': qm[..., :MLA_NOPE],
        'q_rope': q_rope,
        'k_nope': kvm[..., :MLA_NOPE],
        'v': kvm[..., MLA_NOPE:],
        'k_rope': kr,
        'gates': g,
    }


def merge_branches(o_na, o_pool, o_mla, gate_logits, lp):
    B, L, _ = o_pool.shape
    br = jnp.stack([o_na.reshape(B, L, BRANCH_W), o_pool, o_mla.reshape(B, L, BRANCH_W)], axis=2)
    proj = jnp.einsum('blki,kid->blkd', br, lp['w_branch'])
    gates = jax.nn.sigmoid(gate_logits.astype(jnp.float32)).astype(proj.dtype).reshape(B, L, N_BRANCH, D_MODEL)
    return jnp.sum(gates * proj, axis=2) @ lp['w_o']


def conv_ffn(h, lp):
    u = dwconv_centred(h @ lp['w_up'], lp['conv_w'], lp['conv_b'])
    a, b = jnp.split(u, 2, axis=-1)
    return (jax.nn.gelu(a, approximate=True) * b) @ lp['w_down']


def hybrid_layer(x, xc, c, c_ctx, lp, rope_angles, last):
    mod = (jax.nn.silu(c) @ lp['w_ada'] + lp['b_ada'])[:, None, :]
    mod_c = jax.nn.silu(c_ctx) @ lp['w_ada'] + lp['b_ada']
    sh1, sc1, g1, sh2, sc2, g2 = jnp.split(mod, 6, axis=-1)
    csh1, csc1, cg1, csh2, csc2, cg2 = jnp.split(mod_c, 6, axis=-1)

    h = modulate(rms_norm(x, lp['norm_pre1']), sh1, sc1)
    hc = modulate(rms_norm(xc, lp['norm_pre1']), csh1, csc1)
    P = project_stream(h, lp, rope_angles)
    C = project_stream(hc, lp, None)

    o_na = na_latent(P['na_q'], P['na_k'], P['na_v'], C['na_k'], C['na_v'], lp['na_rpb'])
    o_pool = pool_mixer(P['pool_u'], lp['pool_w'], lp['pool_scale'])
    k_nope_all = jnp.concatenate([C['k_nope'], P['k_nope']], axis=1)
    k_rope_all = jnp.concatenate([C['k_rope'], P['k_rope']], axis=1)
    v_all = jnp.concatenate([C['v'], P['v']], axis=1)
    o_mla = mla_attend(P['q_nope'], P['q_rope'], k_nope_all, k_rope_all, v_all)
    y = merge_branches(o_na, o_pool, o_mla, P['gates'], lp)
    x = x + g1 * rms_norm(y, lp['norm_post1'])

    h2 = modulate(rms_norm(x, lp['norm_pre2']), sh2, sc2)
    x = x + g2 * rms_norm(conv_ffn(h2, lp), lp['norm_post2'])

    if last:
        return x, None
    oc_na = dense_attn(C['na_q'], C['na_k'], C['na_v'])
    oc_pool = pool_mixer(C['pool_u'], lp['pool_w'], lp['pool_scale'])
    oc_mla = mla_attend(C['q_nope'], C['q_rope'], C['k_nope'], C['k_rope'], C['v'])
    yc = merge_branches(oc_na, oc_pool, oc_mla, C['gates'], lp)
    xc = xc + cg1 * rms_norm(yc, lp['norm_post1'])
    hc2 = modulate(rms_norm(xc, lp['norm_pre2']), csh2, csc2)
    xc = xc + cg2 * rms_norm(conv_ffn(hc2, lp), lp['norm_post2'])
    return x, xc


def setup_inputs(seed: int = 0) -> dict:
    key = jax.random.key(seed)
    ks = jax.random.split(key, 24)
    f32 = jnp.float32

    def nrm(k, shape, s):
        return jax.random.normal(k, shape, f32) * s

    def gain(k, n):
        return 1.0 + 0.05 * jax.random.normal(k, (DEPTH, n), f32)

    return {
        'x': nrm(ks[0], (BATCH, SEQ, D_MODEL), 1.0),
        'c': nrm(ks[1], (BATCH, D_MODEL), 1.0),
        'ctx': nrm(ks[2], (BATCH, CTX_LEN, D_MODEL), 1.0),
        'c_ctx': nrm(ks[3], (D_MODEL,), 1.0),
        'w_ada': nrm(ks[4], (DEPTH, D_MODEL, 6 * D_MODEL), 0.5 * D_MODEL ** -0.5),
        'b_ada': nrm(ks[5], (DEPTH, 6 * D_MODEL), 0.02),
        'norm_pre1': gain(ks[6], D_MODEL),
        'norm_post1': gain(ks[7], D_MODEL),
        'norm_pre2': gain(ks[8], D_MODEL),
        'norm_post2': gain(ks[9], D_MODEL),
        'w_in': nrm(ks[10], (DEPTH, D_MODEL, IN_COLS), D_MODEL ** -0.5),
        'na_rpb': nrm(ks[11], (DEPTH, NA_HEADS, 2 * NA_WIN_R - 1, 2 * NA_WIN_C - 1), 0.1),
        'pool_w': nrm(ks[12], (DEPTH, len(POOL_WINDOWS), POOL_GROUP, POOL_GROUP), POOL_GROUP ** -0.5),
        'pool_scale': 1.0 + 0.1 * jax.random.normal(ks[13], (DEPTH, POOL_WIDTH), f32),
        'mla_q_norm': gain(ks[14], MLA_Q_RANK),
        'w_uq': nrm(ks[15], (DEPTH, MLA_Q_RANK, MLA_HEADS * (MLA_NOPE + MLA_ROPE)), MLA_Q_RANK ** -0.5),
        'mla_kv_norm': gain(ks[16], MLA_KV_RANK),
        'w_ukv': nrm(ks[17], (DEPTH, MLA_KV_RANK, MLA_HEADS * (MLA_NOPE + MLA_V)), MLA_KV_RANK ** -0.5),
        'w_branch': nrm(ks[18], (DEPTH, N_BRANCH, BRANCH_W, D_MODEL), BRANCH_W ** -0.5),
        'w_o': nrm(ks[19], (DEPTH, D_MODEL, D_MODEL), D_MODEL ** -0.5),
        'w_up': nrm(ks[20], (DEPTH, D_MODEL, 2 * D_FF), D_MODEL ** -0.5),
        'conv_w': nrm(ks[21], (DEPTH, CONV_W, 2 * D_FF), CONV_W ** -0.5),
        'conv_b': nrm(ks[22], (DEPTH, 2 * D_FF), 0.02),
        'w_down': nrm(ks[23], (DEPTH, D_FF, D_MODEL), D_FF ** -0.5),
    }


def reference(x, c, ctx, c_ctx, w_ada, b_ada, norm_pre1, norm_post1, norm_pre2, norm_post2,
              w_in, na_rpb, pool_w, pool_scale, mla_q_norm, w_uq, mla_kv_norm, w_ukv,
              w_branch, w_o, w_up, conv_w, conv_b, w_down):
    rope_angles = axial_angles(x.shape[1], MLA_ROPE)
    xc = ctx
    for l in range(DEPTH):
        lp = {
            'w_ada': w_ada[l], 'b_ada': b_ada[l],
            'norm_pre1': norm_pre1[l], 'norm_post1': norm_post1[l],
            'norm_pre2': norm_pre2[l], 'norm_post2': norm_post2[l],
            'w_in': w_in[l], 'na_rpb': na_rpb[l],
            'pool_w': pool_w[l], 'pool_scale': pool_scale[l],
            'mla_q_norm': mla_q_norm[l], 'w_uq': w_uq[l],
            'mla_kv_norm': mla_kv_norm[l], 'w_ukv': w_ukv[l],
            'w_branch': w_branch[l], 'w_o': w_o[l],
            'w_up': w_up[l], 'conv_w': conv_w[l], 'conv_b': conv_b[l], 'w_down': w_down[l],
        }
        x, xc = hybrid_layer(x, xc, c, c_ctx, lp, rope_angles, l == DEPTH - 1)
    return x
```

```python
import numpy as np
import concourse.bass as bass
import concourse.mybir as mybir
from concourse.bass_utils import run_bass_kernel_spmd
from contextlib import ExitStack

F32 = mybir.dt.float32
BF16 = mybir.dt.bfloat16
AF = mybir.ActivationFunctionType
ALU = mybir.AluOpType

SAME_ENG_SYNC = True
NDMA_SEMS = 8

D = 1024
NH = 8
CTX = 256
DFF = 2816
IN_COLS = 5920
EPS = 1e-6
NEG = -30000.0


class Buf:
    __slots__ = ("name", "multi", "excl", "w_eng", "w_dma", "r_eng", "r_dma")

    def __init__(self, name="", multi=False):
        self.name = name
        self.multi = multi
        self.excl = False
        self.w_eng = {}
        self.w_dma = []
        self.r_eng = {}
        self.r_dma = []


class TT:
    __slots__ = ("t", "b")

    def __init__(self, t, name="", multi=False):
        self.t = t
        self.b = Buf(name, multi)


class Ins:
    __slots__ = ("eng", "fn", "deps", "marked", "sem", "val", "is_dma", "idx", "thr")


def _b(x):
    return x.b if isinstance(x, TT) else x


class Prog:
    ENGS = ("tensor", "vector", "scalar", "gpsimd", "sync")

    def __init__(self, nc):
        self.nc = nc
        self.streams = {e: [] for e in self.ENGS}
        self.n = 0
        self.bar_pending = {}
        self.dma_since = []
        self.last_c = {}

    def barrier(self):
        deps = list(self.last_c.values()) + list(self.dma_since)
        self.dma_since = []
        for e in self.ENGS:
            self.bar_pending[e] = list(self.bar_pending.get(e, [])) + deps

    def op(self, eng, fn, reads=(), writes=(), dma=False):
        ins = Ins()
        ins.eng = eng
        ins.fn = fn
        ins.is_dma = dma
        ins.marked = False
        ins.idx = self.n
        ins.sem = None
        ins.val = 0
        ins.thr = None
        self.n += 1
        deps = {}

        def add(d):
            if not d.is_dma:
                if d.eng == eng and not dma and (eng == "tensor" or not SAME_ENG_SYNC):
                    return
                k = ("c", d.eng)
                if k not in deps or deps[k].idx < d.idx:
                    deps[k] = d
            else:
                deps[("d", d.idx)] = d

        reads = [_b(x) for x in reads]
        writes = [_b(x) for x in writes]
        if self.bar_pending.get(eng):
            for d in self.bar_pending[eng]:
                add(d)
            self.bar_pending[eng] = []
        if dma:
            self.dma_since.append(ins)
        else:
            self.last_c[eng] = ins
        for b in reads:
            for d in b.w_eng.values():
                add(d)
            for d in b.w_dma:
                add(d)
            if b.excl:
                for e2, d in b.r_eng.items():
                    if e2 != eng:
                        add(d)
        for b in writes:
            if not b.multi:
                for d in b.w_eng.values():
                    add(d)
                for d in b.w_dma:
                    add(d)
            for d in b.r_eng.values():
                add(d)
            for d in b.r_dma:
                add(d)
        ins.deps = list(deps.values())
        for d in ins.deps:
            d.marked = True
        for b in reads:
            if dma:
                b.r_dma.append(ins)
            else:
                b.r_eng[eng] = ins
        for b in writes:
            if not b.multi:
                b.w_eng = {}
                b.w_dma = []
                b.r_eng = {}
                b.r_dma = []
            if dma:
                b.w_dma.append(ins)
            else:
                b.w_eng[eng] = ins
        self.streams[eng].append(ins)
        return ins

    def emit(self, final_wait_eng="sync"):
        nc = self.nc
        with ExitStack() as es:
            tick = {}
            for e in ("tensor", "vector", "scalar", "gpsimd"):
                tick[e] = es.enter_context(nc.semaphore("tk_" + e))
            dsem = {}
            for e in ("sync", "gpsimd", "scalar"):
                dsem[e] = [es.enter_context(nc.semaphore("d_%s%d" % (e, i))) for i in range(NDMA_SEMS)]
            all_dma = []
            for e, st in self.streams.items():
                cnt = 0
                nd = 0
                lastslot = [None] * NDMA_SEMS
                for ins in st:
                    if ins.is_dma:
                        slot = nd % NDMA_SEMS
                        prev = lastslot[slot]
                        ins.sem = dsem[e][slot]
                        ins.val = (prev.val if prev is not None else 0) + 16
                        ins.thr = prev
                        lastslot[slot] = ins
                        nd += 1
                        all_dma.append(ins)
                    elif ins.marked:
                        cnt += 1
                        ins.sem = tick[e]
                        ins.val = cnt
            block = es.enter_context(nc.Block())
            counts = {}

            def run_stream(ename, eobj):
                waited = {}
                n = 0

                def w(sem, val):
                    k = id(sem)
                    if waited.get(k, 0) >= val:
                        return 0
                    waited[k] = val
                    eobj.wait_ge(sem, val)
                    return 1

                for ins in self.streams[ename]:
                    if ins.thr is not None:
                        n += w(ins.thr.sem, ins.thr.val)
                    for d in ins.deps:
                        n += w(d.sem, d.val)
                    bi = ins.fn(eobj)
                    n += 1
                    if ins.is_dma:
                        bi.then_inc(ins.sem, 16)
                    elif ins.marked:
                        bi.then_inc(ins.sem, 1)
                if ename == final_wait_eng:
                    for d in reversed(all_dma):
                        n += w(d.sem, d.val)
                counts[ename] = n

            @block.sync
            def _(e):
                run_stream("sync", e)

            @block.tensor
            def _(e):
                run_stream("tensor", e)

            @block.vector
            def _(e):
                run_stream("vector", e)

            @block.scalar
            def _(e):
                run_stream("scalar", e)

            @block.gpsimd
            def _(e):
                run_stream("gpsimd", e)

            self.counts = counts


class KB:
    def __init__(self, S, L, debug=()):
        self.S = S
        self.L = L
        self.T = S + CTX
        self.NT = S // 128
        self.NM = S // 512
        self.R = S // 64
        self.debug = set(debug)
        self.nc = bass.Bass("TRN2", target_bir_lowering=False)
        self.p = Prog(self.nc)
        self.inputs = {}
        self.uid = 0
        self.stop_after = None

    def inp(self, name, shape, dt=F32):
        t = self.nc.dram_tensor(name, list(shape), dt, kind="ExternalInput").ap()
        tt = TT(t, name, multi=True)
        self.inputs[name] = tt
        return tt

    def dram(self, name, shape, dt):
        kind = "ExternalOutput" if name in self.debug else "Internal"
        t = self.nc.dram_tensor(name, list(shape), dt, kind=kind).ap()
        return TT(t, name, multi=True)

    def sb(self, es, name, shape, dt):
        self.uid += 1
        t = es.enter_context(self.nc.sbuf_tensor("%s_%d" % (name, self.uid), list(shape), dt))
        return TT(t, name)

    def psum_banks(self, es, n=8):
        es.callback(self.p.barrier)
        banks = []
        for i in range(n):
            self.uid += 1
            t = es.enter_context(self.nc.psum_tensor("ps%d_%d" % (i, self.uid), [128, 512], F32))
            banks.append(TT(t, "ps%d" % i))
            banks[-1].b.excl = True
        return banks

    def mm(self, out, lhsT, rhs, start=True, stop=True, r=(), w=()):
        self.p.op("tensor", lambda e: e.matmul(out, lhsT=lhsT, rhs=rhs, start=start, stop=stop), r, w)

    def tr(self, out, in_, ident, r=(), w=()):
        self.p.op("tensor", lambda e: e.transpose(out=out, in_=in_, identity=ident), r, w)

    def act(self, out, in_, func, r=(), w=(), **kw):
        self.p.op("scalar", lambda e: e.activation(out=out, in_=in_, func=func, **kw), r, w)

    def tt(self, eng, out, in0, in1, op, r=(), w=()):
        self.p.op(eng, lambda e: e.tensor_tensor(out=out, in0=in0, in1=in1, op=op), r, w)

    def ts(self, eng, out, in0, s1, s2, op0, op1=None, r=(), w=()):
        if op1 is None:
            self.p.op(eng, lambda e: e.tensor_scalar(out=out, in0=in0, scalar1=s1, scalar2=None, op0=op0), r, w)
        else:
            self.p.op(eng, lambda e: e.tensor_scalar(out=out, in0=in0, scalar1=s1, scalar2=s2, op0=op0, op1=op1), r, w)

    def stt(self, eng, out, in0, scalar, in1, op0, op1, r=(), w=()):
        self.p.op(eng, lambda e: e.scalar_tensor_tensor(out=out, in0=in0, scalar=scalar, in1=in1, op0=op0, op1=op1), r, w)

    def copy(self, eng, out, in_, r=(), w=()):
        if eng == "scalar":
            self.p.op(eng, lambda e: e.copy(out=out, in_=in_), r, w)
        else:
            self.p.op(eng, lambda e: e.tensor_copy(out=out, in_=in_), r, w)

    def recip(self, out, in_, r=(), w=()):
        self.p.op("vector", lambda e: e.reciprocal(out=out, in_=in_), r, w)

    def memset(self, eng, ap, val, w=()):
        self.p.op(eng, lambda e: e.memset(ap, val), (), w)

    def dma(self, eng, out, in_, r=(), w=()):
        self.p.op(eng, lambda e: e.dma_start(out=out, in_=in_), r, w, dma=True)

    def declare_inputs(self):
        S, L, T = self.S, self.L, self.T
        i = self.inp
        i("x", [S, D]); i("ctx", [CTX, D]); i("ccT", [128, 8, 2])
        i("w_ada", [L, D, 6 * D]); i("b_adaT", [L, 128, 48]); i("b_ada", [L, 1, 6 * D])
        i("normsT", [L, 4, 128, 8]); i("norms", [L, 4, 1, D])
        i("w_in", [L, D, IN_COLS]); i("w_kr96", [L, D, 192])
        i("w_uq2", [L, 512, 1536]); i("w_ukv_k", [L, 256, 512]); i("w_ukv_v", [L, 256, 512])
        i("qnT", [L, 128, 4]); i("kvnT", [L, 128, 2])
        i("pool_w", [L, 4, 128, 128]); i("pool_scaleT", [L, 128, 4])
        i("w_branch", [L, 3, 512, D]); i("w_o", [L, D, D]); i("w_up", [L, D, 2 * DFF])
        i("conv_wT", [L, 128, 44, 3]); i("conv_bT", [L, 128, 44]); i("w_down", [L, DFF, D])
        i("na_blk", [L, NH, 16, 64, 64])
        i("ropeq", [2, 32, S]); i("ropek", [2, 32, S]); i("pool_inv", [4, 1, T])
        i("ident", [128, 128]); i("sel", [2, 256])
        out = self.nc.dram_tensor("out", [S, D], F32, kind="ExternalOutput").ap()
        self.out = TT(out, "out", multi=True)

    def mod_dbg(self, mod, l):
        if "moddbg_%d" % l not in self.debug:
            return
        d = self.dram("moddbg_%d" % l, [128, 96 + 32 + 4 * D], F32)
        self.dma("gpsimd", d.t[:, 0:96], mod["modT"].t[:].rearrange("p c w -> p (c w)"), r=[mod["modT"]], w=[d])
        self.dma("gpsimd", d.t[:, 96:112], mod["A1"].t[:].rearrange("p c w -> p (c w)"), r=[mod["A1"]], w=[d])
        self.dma("gpsimd", d.t[:, 112:128], mod["A2"].t[:].rearrange("p c w -> p (c w)"), r=[mod["A2"]], w=[d])
        for i, g in enumerate(mod["G1"] + mod["G2"]):
            self.dma("gpsimd", d.t[:, 128 + i * D:128 + (i + 1) * D], g.t[:], r=[g], w=[d])

    def segs(self, last):
        s = [(0, 0, self.S)]
        if not last:
            s.append((1, self.S, CTX))
        return s

    def macro_tiles(self, start, length):
        out = []
        o = 0
        while o < length:
            n = min(512, length - o)
            out.append((start + o, n))
            o += n
        return out

    def build(self):
        self.declare_inputs()
        I = self.inputs
        S, T, L = self.S, self.T, self.L
        xsrc, csrc = I["x"], I["ctx"]
        with ExitStack() as gs:
            self.ident = self.sb(gs, "ident", [128, 128], BF16)
            self.dma("gpsimd", self.ident.t[:], I["ident"].t, r=[I["ident"]], w=[self.ident])
            self.ones = self.sb(gs, "ones", [128, 128], BF16)
            self.memset("vector", self.ones.t[:], 1.0, w=[self.ones])
            self.onesf = self.sb(gs, "onesf", [128, 64], F32)
            self.memset("vector", self.onesf.t[:], 1.0, w=[self.onesf])
            for l in range(L):
                last = (l == L - 1)
                sc = {}
                for nm, shp, dt in [("hT", [D, T], BF16), ("naq", [NH, 64, T], BF16), ("nak", [NH, 64, T], BF16),
                                    ("nav", [T, 512], BF16), ("poolu", [512, T], F32), ("qm", [NH, 96, T], BF16),
                                    ("km", [NH, 96, T], BF16), ("vm", [NH, 128, T // 128, 64], BF16),
                                    ("ona", [512, T], BF16), ("opool", [512, T], BF16), ("omla", [512, T], BF16),
                                    ("xmid", [T, D], F32), ("h2T", [D, T], BF16), ("x1", [S, D], F32),
                                    ("xc1", [CTX, D], F32)]:
                    sc[nm] = self.dram("%s_%d" % (nm, l), shp, dt)
                with ExitStack() as ls:
                    stop = self.stop_after
                    mod = self.stage_ada(ls, l)
                    self.mod_dbg(mod, l)
                    if stop != "ada":
                        self.stage_proj(l, mod, xsrc, csrc, sc, last)
                    if stop not in ("ada", "proj"):
                        self.stage_pool(l, sc, last)
                    if stop not in ("ada", "proj", "pool"):
                        self.stage_na(l, sc, last)
                    if stop not in ("ada", "proj", "pool", "na"):
                        self.stage_mla(l, sc, last)
                    xo = self.out if last else sc["x1"]
                    if stop not in ("ada", "proj", "pool", "na", "mla"):
                        self.stage_merge(l, mod, xsrc, csrc, sc, last)
                    if stop not in ("ada", "proj", "pool", "na", "mla", "merge"):
                        self.stage_ffn(l, mod, sc, xo, sc["xc1"], last)
                xsrc, csrc = sc["x1"], sc["xc1"]
            self.p.emit()
        return self.nc

    def stage_ada(self, ls, l):
        I = self.inputs
        mod = {}
        modT = self.sb(ls, "modT", [128, 48, 2], F32)
        A1 = self.sb(ls, "A1", [128, 8, 2], F32)
        A2 = self.sb(ls, "A2", [128, 8, 2], F32)
        G1 = [self.sb(ls, "G1_%d" % w, [128, D], F32) for w in range(2)]
        G2 = [self.sb(ls, "G2_%d" % w, [128, D], F32) for w in range(2)]
        with ExitStack() as es:
            ps = self.psum_banks(es, 4)
            ccT = self.sb(es, "ccT", [128, 8, 2], F32)
            scT = self.sb(es, "scT", [128, 8, 2], F32)
            badaT = self.sb(es, "badaT", [128, 48], F32)
            nT = self.sb(es, "nT", [128, 4, 8], F32)
            brow = self.sb(es, "brow", [2, 6 * D], F32)
            modrow = self.sb(es, "modrow", [2, 6 * D], F32)
            sel = self.sb(es, "sel", [2, 256], F32)
            npost = [self.sb(es, "npost%d" % i, [128, D], F32) for i in range(2)]
            was = [self.sb(es, "wa%d" % i, [128, 8, 512], F32) for i in range(2)]
            self.dma("sync", ccT.t[:], I["ccT"].t, r=[I["ccT"]], w=[ccT])
            self.dma("sync", badaT.t[:], I["b_adaT"].t[l], r=[I["b_adaT"]], w=[badaT])
            self.dma("sync", nT.t[:], I["normsT"].t[l].rearrange("f p k -> p f k"), r=[I["normsT"]], w=[nT])
            self.dma("sync", brow.t[:], I["b_ada"].t[l].partition_broadcast(2), r=[I["b_ada"]], w=[brow])
            self.dma("sync", sel.t[:], I["sel"].t, r=[I["sel"]], w=[sel])
            self.dma("sync", npost[0].t[:], I["norms"].t[l, 1].partition_broadcast(128), r=[I["norms"]], w=[npost[0]])
            self.dma("sync", npost[1].t[:], I["norms"].t[l, 3].partition_broadcast(128), r=[I["norms"]], w=[npost[1]])
            self.act(scT.t[:], ccT.t[:], AF.Silu, r=[ccT], w=[scT])
            pcol = ps[0]
            for cb in range(12):
                wa = was[cb % 2]
                self.dma("sync", wa.t[:], I["w_ada"].t[l][:, cb * 512:(cb + 1) * 512].rearrange("(k p) n -> p k n", p=128),
                         r=[I["w_ada"]], w=[wa])
                prow = ps[1 + cb % 2]
                for k in range(8):
                    self.mm(prow.t[0:2, :], scT.t[:, k, :], wa.t[:, k, :], start=(k == 0), stop=(k == 7), r=[scT, wa], w=[prow])
                self.tt("vector", modrow.t[:, cb * 512:(cb + 1) * 512], prow.t[0:2, :], brow.t[:, cb * 512:(cb + 1) * 512], ALU.add,
                        r=[prow, brow], w=[modrow])
                for cc in range(4):
                    c = cb * 4 + cc
                    for k in range(8):
                        self.mm(pcol.t[:, 2 * c:2 * c + 2], wa.t[:, k, cc * 128:(cc + 1) * 128], scT.t[:, k, :],
                                start=(k == 0), stop=(k == 7), r=[scT, wa], w=[pcol])
            pv = pcol.t[:, 0:96].rearrange("p (c w) -> p c w", w=2)
            for w in range(2):
                self.tt("vector", modT.t[:, :, w], pv[:, :, w], badaT.t[:], ALU.add, r=[pcol, badaT], w=[modT])
            for w in range(2):
                self.stt("vector", A1.t[:, :, w], modT.t[:, 8:16, w], 1.0, nT.t[:, 0, :], ALU.add, ALU.mult, r=[modT, nT], w=[A1])
                self.stt("vector", A2.t[:, :, w], modT.t[:, 32:40, w], 1.0, nT.t[:, 2, :], ALU.add, ALU.mult, r=[modT, nT], w=[A2])
            for (G, col0, npi) in ((G1, 2 * D, 0), (G2, 5 * D, 1)):
                for w in range(2):
                    for hf in range(2):
                        pb = ps[1 + hf]
                        self.mm(pb.t[:], sel.t[0:2, w * 128:(w + 1) * 128], modrow.t[0:2, col0 + hf * 512:col0 + (hf + 1) * 512],
                                r=[sel, modrow], w=[pb])
                        self.tt("vector", G[w].t[:, hf * 512:(hf + 1) * 512], pb.t[:], npost[npi].t[:, hf * 512:(hf + 1) * 512], ALU.mult,
                                r=[pb, npost[npi]], w=[G[w]])
        mod.update(modT=modT, A1=A1, A2=A2, G1=G1, G2=G2)
        return mod

    def norm_transpose(self, xt, A, sh_c0, modT, w, hT, col0, tmp, pT):
        junk, ss, rstd, xn = tmp["junk"], tmp["ss"], tmp["rstd"], tmp["xn"]
        self.act(junk.t[:], xt.t[:], AF.Square, r=[xt], w=[junk, ss], accum_out=ss.t[:, 0:1])
        self.act(rstd.t[:, 0:1], ss.t[:, 0:1], AF.Sqrt, r=[ss], w=[rstd], bias=EPS, scale=1.0 / D)
        self.recip(rstd.t[:, 0:1], rstd.t[:, 0:1], r=[rstd], w=[rstd])
        self.ts("vector", xn.t[:], xt.t[:], rstd.t[:, 0:1], None, ALU.mult, r=[xt, rstd], w=[xn])
        pv = pT.t[:].bitcast(BF16).rearrange("p (c n) -> p c n", n=128)
        for c in range(8):
            self.tr(pv[:, c, :], xn.t[:, c * 128:(c + 1) * 128], self.ident.t[:], r=[xn, self.ident], w=[pT])
        self.ntc = getattr(self, "ntc", 0) + 1
        for c in range(8):
            o = hT.t[:, c, col0:col0 + 128]
            if self.ntc % 2 == 0:
                self.act(o, pv[:, c, :], AF.Identity, r=[pT, A, modT], w=[hT], scale=A.t[:, c, w:w + 1],
                         bias=modT.t[:, sh_c0 + c, w:w + 1])
            else:
                self.ts("vector", o, pv[:, c, :], A.t[:, c, w:w + 1], modT.t[:, sh_c0 + c, w:w + 1], ALU.mult, ALU.add,
                        r=[pT, A, modT], w=[hT])

    def stage_proj(self, l, mod, xsrc, csrc, sc, last):
        I = self.inputs
        S, T = self.S, self.T
        with ExitStack() as es:
            ps = self.psum_banks(es, 8)
            rot = [0]

            def nb():
                rot[0] = (rot[0] % 7) + 1
                return ps[rot[0]]

            win = self.sb(es, "win", [128, 8, 2848], BF16)
            for k in range(8):
                self.dma("gpsimd", win.t[:, k, :], I["w_in"].t[l][k * 128:(k + 1) * 128, 0:2848], r=[I["w_in"]], w=[win])
            wkr = self.sb(es, "wkr", [128, 8, 192], BF16)
            self.dma("gpsimd", wkr.t[:], I["w_kr96"].t[l].rearrange("(k p) n -> p k n", p=128), r=[I["w_kr96"]], w=[wkr])
            wuq = self.sb(es, "wuq", [128, 4, 1536], BF16)
            self.dma("gpsimd", wuq.t[:], I["w_uq2"].t[l].rearrange("(k p) n -> p k n", p=128), r=[I["w_uq2"]], w=[wuq])
            wukk = self.sb(es, "wukk", [128, 2, 512], BF16)
            self.dma("gpsimd", wukk.t[:], I["w_ukv_k"].t[l].rearrange("(k p) n -> p k n", p=128), r=[I["w_ukv_k"]], w=[wukk])
            wukv = self.sb(es, "wukv", [128, 2, 512], BF16)
            self.dma("gpsimd", wukv.t[:], I["w_ukv_v"].t[l].rearrange("(k p) n -> p k n", p=128), r=[I["w_ukv_v"]], w=[wukv])
            qnT = self.sb(es, "qnT", [128, 4], F32)
            self.dma("sync", qnT.t[:], I["qnT"].t[l], r=[I["qnT"]], w=[qnT])
            kvnT = self.sb(es, "kvnT", [128, 2], F32)
            self.dma("sync", kvnT.t[:], I["kvnT"].t[l], r=[I["kvnT"]], w=[kvnT])
            xts = [self.sb(es, "xt%d" % i, [128, D], F32) for i in range(3)]
            tmps = [dict(junk=self.sb(es, "junk", [128, D], BF16), ss=self.sb(es, "ss", [128, 1], F32),
                         rstd=self.sb(es, "rstd", [128, 1], F32), xn=self.sb(es, "xn", [128, D], BF16)) for i in range(2)]
            hTs = [self.sb(es, "hT%d" % i, [128, 8, 512], BF16) for i in range(2)]
            stg = [self.sb(es, "stg%d" % i, [128, 512], BF16) for i in range(6)]
            stgf = [self.sb(es, "stgf%d" % i, [128, 512], F32) for i in range(2)]
            cqT = self.sb(es, "cqT", [128, 4, 512], F32)
            sq = self.sb(es, "sq", [128, 4, 512], BF16)
            ckvT = self.sb(es, "ckvT", [128, 2, 512], F32)
            sqkv = self.sb(es, "sqkv", [128, 2, 512], BF16)
            rbc = [self.sb(es, "rbc%d" % i, [128, 512], F32) for i in range(2)]
            cqn = self.sb(es, "cqn", [128, 4, 512], BF16)
            ckvn = self.sb(es, "ckvn", [128, 2, 512], BF16)
            rq = [self.sb(es, "rq%d" % i, [96, 2, 512], F32) for i in range(2)]
            rk = [self.sb(es, "rk%d" % i, [96, 2, 512], F32) for i in range(2)]
            rt = [self.sb(es, "rt%d" % i, [96, 512], F32) for i in range(4)]
            cnt = dict(stg=0, stgf=0, x=0, rt=0, ev=0)

            def nstg():
                cnt["stg"] += 1
                return stg[cnt["stg"] % 6]

            def evac(out_ap, in_ap, r, w, scale=None):
                cnt["ev"] += 1
                if cnt["ev"] % 2 == 0:
                    if scale is None:
                        self.copy("scalar", out_ap, in_ap, r=r, w=w)
                    else:
                        self.act(out_ap, in_ap, AF.Identity, r=r, w=w, scale=scale)
                else:
                    if scale is None:
                        self.copy("vector", out_ap, in_ap, r=r, w=w)
                    else:
                        self.ts("vector", out_ap, in_ap, scale, None, ALU.mult, r=r, w=w)

            mi = 0
            for (w, s0, slen) in [(0, 0, S), (1, S, CTX)]:
                src = xsrc if w == 0 else csrc
                for (c0, n) in self.macro_tiles(s0, slen):
                    mi += 1
                    nt = n // 128
                    hT = hTs[mi % 2]
                    for i in range(nt):
                        cnt["x"] += 1
                        xt = xts[cnt["x"] % 3]
                        r0 = c0 - s0 + i * 128
                        self.dma("sync", xt.t[:], src.t[r0:r0 + 128, :], r=[src], w=[xt])
                        self.norm_transpose(xt, mod["A1"], 0, mod["modT"], w, hT, i * 128, tmps[cnt["x"] % 2], ps[0])
                    self.dma("gpsimd", sc["hT"].t[:, c0:c0 + n].rearrange("(k p) t -> p k t", p=128), hT.t[:, :, 0:n], r=[hT], w=[sc["hT"]])

                    def fm(wt, col0, M):
                        pb = nb()
                        for k in range(8):
                            self.mm(pb.t[0:M, 0:n], wt.t[:, k, col0:col0 + M], hT.t[:, k, 0:n], start=(k == 0), stop=(k == 7),
                                    r=[wt, hT], w=[pb])
                        return pb
                    if w == 0 or not last:
                        for cc in range(4):
                            pb = fm(win, cc * 128, 128)
                            st = nstg()
                            evac(st.t[:, 0:n], pb.t[:, 0:n], [pb], [st], scale=0.125)
                            self.dma("gpsimd", sc["naq"].t[2 * cc:2 * cc + 2, :, c0:c0 + n].rearrange("h d t -> (h d) t"), st.t[:, 0:n],
                                     r=[st], w=[sc["naq"]])
                    for cc in range(4):
                        pb = fm(win, 512 + cc * 128, 128)
                        st = nstg()
                        evac(st.t[:, 0:n], pb.t[:, 0:n], [pb], [st])
                        self.dma("gpsimd", sc["nak"].t[2 * cc:2 * cc + 2, :, c0:c0 + n].rearrange("h d t -> (h d) t"), st.t[:, 0:n],
                                 r=[st], w=[sc["nak"]])
                    for i in range(nt):
                        pb = nb()
                        for k in range(8):
                            self.mm(pb.t[:, :], hT.t[:, k, i * 128:(i + 1) * 128], win.t[:, k, 1024:1536], start=(k == 0), stop=(k == 7),
                                    r=[win, hT], w=[pb])
                        st = nstg()
                        evac(st.t[:], pb.t[:], [pb], [st])
                        self.dma("gpsimd", sc["nav"].t[c0 + i * 128:c0 + (i + 1) * 128, :], st.t[:], r=[st], w=[sc["nav"]])
                    if w == 0 or not last:
                        for cc in range(4):
                            pb = fm(win, 1536 + cc * 128, 128)
                            cnt["stgf"] += 1
                            st = stgf[cnt["stgf"] % 2]
                            evac(st.t[:, 0:n], pb.t[:, 0:n], [pb], [st])
                            self.dma("gpsimd", sc["poolu"].t[cc * 128:(cc + 1) * 128, c0:c0 + n], st.t[:, 0:n], r=[st], w=[sc["poolu"]])
                    if w == 0:
                        rqt, rkt = rq[mi % 2], rk[mi % 2]
                        self.dma("sync", rqt.t[64:96, :, 0:n], I["ropeq"].t[:, :, c0:c0 + n].rearrange("v d t -> d v t"), r=[I["ropeq"]], w=[rqt])
                        self.dma("sync", rkt.t[64:96, :, 0:n], I["ropek"].t[:, :, c0:c0 + n].rearrange("v d t -> d v t"), r=[I["ropek"]], w=[rkt])

                    def rope_combine(p0, p1, tab, out_ap, outbuf, scale_ctx):
                        if w == 0:
                            cnt["rt"] += 1
                            t1 = rt[cnt["rt"] % 4]
                            cnt["rt"] += 1
                            t2 = rt[cnt["rt"] % 4]
                            self.tt("vector", t1.t[64:96, 0:n], p0.t[64:96, 0:n], tab.t[64:96, 0, 0:n], ALU.mult, r=[p0, tab], w=[t1])
                            self.tt("vector", t2.t[64:96, 0:n], p1.t[64:96, 0:n], tab.t[64:96, 1, 0:n], ALU.mult, r=[p1, tab], w=[t2])
                            self.tt("gpsimd", out_ap, t1.t[64:96, 0:n], t2.t[64:96, 0:n], ALU.add, r=[t1, t2], w=[outbuf])
                        else:
                            if scale_ctx is None:
                                self.copy("vector", out_ap, p0.t[64:96, 0:n], r=[p0], w=[outbuf])
                            else:
                                self.act(out_ap, p0.t[64:96, 0:n], AF.Identity, r=[p0], w=[outbuf], scale=scale_ctx)
                    p0 = fm(wkr, 0, 96)
                    p1 = fm(wkr, 96, 96) if w == 0 else None
                    st = nstg()
                    rope_combine(p0, p1, rk[mi % 2], st.t[64:96, 0:n], st, None)
                    for h in range(NH):
                        self.dma("gpsimd", sc["km"].t[h, 64:96, c0:c0 + n], st.t[64:96, 0:n], r=[st], w=[sc["km"]])
                    if w == 0 or not last:
                        for cc in range(4):
                            pb = fm(win, 2048 + cc * 128, 128)
                            self.copy("vector", cqT.t[:, cc, 0:n], pb.t[:, 0:n], r=[pb], w=[cqT])
                            self.act(sq.t[:, cc, 0:n], cqT.t[:, cc, 0:n], AF.Square, r=[cqT], w=[sq])
                        pb = nb()
                        for cc in range(4):
                            self.mm(pb.t[:, 0:n], self.ones.t[:], sq.t[:, cc, 0:n], start=(cc == 0), stop=(cc == 3), r=[self.ones, sq], w=[pb])
                        self.act(rbc[0].t[:, 0:n], pb.t[:, 0:n], AF.Sqrt, r=[pb], w=[rbc[0]], bias=EPS, scale=1.0 / 512)
                        self.recip(rbc[0].t[:, 0:n], rbc[0].t[:, 0:n], r=[rbc[0]], w=[rbc[0]])
                        for cc in range(4):
                            self.stt("vector", cqn.t[:, cc, 0:n], cqT.t[:, cc, 0:n], qnT.t[:, cc:cc + 1], rbc[0].t[:, 0:n],
                                     ALU.mult, ALU.mult, r=[cqT, qnT, rbc[0]], w=[cqn])
                        qscale = float(96 ** -0.5)
                        for h in range(NH):
                            p0 = nb()
                            for cc in range(4):
                                self.mm(p0.t[0:96, 0:n], wuq.t[:, cc, h * 96:(h + 1) * 96], cqn.t[:, cc, 0:n], start=(cc == 0), stop=(cc == 3),
                                        r=[wuq, cqn], w=[p0])
                            p1 = None
                            if w == 0:
                                p1 = nb()
                                for cc in range(4):
                                    self.mm(p1.t[0:96, 0:n], wuq.t[:, cc, 768 + h * 96:768 + (h + 1) * 96], cqn.t[:, cc, 0:n],
                                            start=(cc == 0), stop=(cc == 3), r=[wuq, cqn], w=[p1])
                            st = nstg()
                            self.act(st.t[0:64, 0:n], p0.t[0:64, 0:n], AF.Identity, r=[p0], w=[st], scale=qscale)
                            rope_combine(p0, p1, rq[mi % 2], st.t[64:96, 0:n], st, qscale)
                            self.dma("gpsimd", sc["qm"].t[h, :, c0:c0 + n], st.t[0:96, 0:n], r=[st], w=[sc["qm"]])
                    for cc in range(2):
                        pb = fm(win, 2560 + cc * 128, 128)
                        self.copy("vector", ckvT.t[:, cc, 0:n], pb.t[:, 0:n], r=[pb], w=[ckvT])
                        self.act(sqkv.t[:, cc, 0:n], ckvT.t[:, cc, 0:n], AF.Square, r=[ckvT], w=[sqkv])
                    pb = nb()
                    for cc in range(2):
                        self.mm(pb.t[:, 0:n], self.ones.t[:], sqkv.t[:, cc, 0:n], start=(cc == 0), stop=(cc == 1), r=[self.ones, sqkv], w=[pb])
                    self.act(rbc[1].t[:, 0:n], pb.t[:, 0:n], AF.Sqrt, r=[pb], w=[rbc[1]], bias=EPS, scale=1.0 / 256)
                    self.recip(rbc[1].t[:, 0:n], rbc[1].t[:, 0:n], r=[rbc[1]], w=[rbc[1]])
                    for cc in range(2):
                        self.stt("vector", ckvn.t[:, cc, 0:n], ckvT.t[:, cc, 0:n], kvnT.t[:, cc:cc + 1], rbc[1].t[:, 0:n],
                                 ALU.mult, ALU.mult, r=[ckvT, kvnT, rbc[1]], w=[ckvn])
                    for hp in range(4):
                        pb = nb()
                        for cc in range(2):
                            self.mm(pb.t[:, 0:n], wukk.t[:, cc, hp * 128:(hp + 1) * 128], ckvn.t[:, cc, 0:n], start=(cc == 0), stop=(cc == 1),
                                    r=[wukk, ckvn], w=[pb])
                        st = nstg()
                        evac(st.t[:, 0:n], pb.t[:, 0:n], [pb], [st])
                        for j in range(2):
                            self.dma("gpsimd", sc["km"].t[2 * hp + j, 0:64, c0:c0 + n], st.t[j * 64:(j + 1) * 64, 0:n], r=[st], w=[sc["km"]])
                    for i in range(nt):
                        pb = nb()
                        for cc in range(2):
                            self.mm(pb.t[:, :], ckvn.t[:, cc, i * 128:(i + 1) * 128], wukv.t[:, cc, :], start=(cc == 0), stop=(cc == 1),
                                    r=[wukv, ckvn], w=[pb])
                        st = nstg()
                        evac(st.t[:], pb.t[:], [pb], [st])
                        tt_i = (c0 + i * 128) // 128
                        self.dma("gpsimd", sc["vm"].t[:, :, tt_i, :].rearrange("h p d -> p h d"),
                                 st.t[:].rearrange("p (h d) -> p h d", d=64), r=[st], w=[sc["vm"]])

    def stage_pool(self, l, sc, last):
        I = self.inputs
        S = self.S
        with ExitStack() as es:
            ps = self.psum_banks(es, 4)
            LM = S
            U = self.sb(es, "pU", [128, LM + 16], F32)
            A = self.sb(es, "pA", [128, LM + 16], F32)
            Bt = self.sb(es, "pB", [128, LM + 16], F32)
            inv = self.sb(es, "pinv", [128, LM], F32)
            dd = self.sb(es, "pd", [128, LM], BF16)
            pw = self.sb(es, "pw", [128, 4, 128], BF16)
            self.dma("gpsimd", pw.t[:], I["pool_w"].t[l].rearrange("g i o -> i g o"), r=[I["pool_w"]], w=[pw])
            pscT = self.sb(es, "pscT", [128, 4], F32)
            self.dma("sync", pscT.t[:], I["pool_scaleT"].t[l], r=[I["pool_scaleT"]], w=[pscT])
            stg = [self.sb(es, "pstg%d" % i, [128, 512], BF16) for i in range(3)]
            k = 0
            for (w, s0, Ln) in self.segs(last):
                for g in range(4):
                    self.memset("vector", U.t[:, 0:8], 0.0, w=[U])
                    self.memset("vector", U.t[:, 8 + Ln:16 + Ln], 0.0, w=[U])
                    self.dma("sync", U.t[:, 8:8 + Ln], sc["poolu"].t[g * 128:(g + 1) * 128, s0:s0 + Ln], r=[sc["poolu"]], w=[U])
                    self.dma("sync", inv.t[:, 0:Ln], I["pool_inv"].t[g, :, s0:s0 + Ln].partition_broadcast(128), r=[I["pool_inv"]], w=[inv])
                    self.tt("vector", A.t[:, 1:Ln + 15], U.t[:, 0:Ln + 14], U.t[:, 1:Ln + 15], ALU.add, r=[U], w=[A])
                    cur = A
                    oth = Bt
                    lo, hi, sh = 1, Ln + 15, 1
                    for step in range(g):
                        lo2, hi2 = lo + sh, hi - sh
                        self.tt("gpsimd" if step % 2 == 0 else "vector", oth.t[:, lo2:hi2], cur.t[:, lo2 - sh:hi2 - sh], cur.t[:, lo2 + sh:hi2 + sh], ALU.add,
                                r=[cur], w=[oth])
                        cur, oth = oth, cur
                        lo, hi, sh = lo2, hi2, sh * 2
                    self.tt("vector", oth.t[:, 8:8 + Ln], cur.t[:, 8:8 + Ln], inv.t[:, 0:Ln], ALU.mult, r=[cur, inv], w=[oth])
                    self.tt("gpsimd", dd.t[:, 0:Ln], oth.t[:, 8:8 + Ln], U.t[:, 8:8 + Ln], ALU.subtract, r=[oth, U], w=[dd])
                    for (c0, n) in self.macro_tiles(0, Ln):
                        k += 1
                        pb = ps[k % 4]
                        self.mm(pb.t[:, 0:n], pw.t[:, g, :], dd.t[:, c0:c0 + n], r=[pw, dd], w=[pb])
                        st = stg[k % 3]
                        self.act(st.t[:, 0:n], pb.t[:, 0:n], AF.Identity, r=[pb, pscT], w=[st], scale=pscT.t[:, g:g + 1])
                        self.dma("gpsimd", sc["opool"].t[g * 128:(g + 1) * 128, s0 + c0:s0 + c0 + n], st.t[:, 0:n], r=[st], w=[sc["opool"]])

    def attn(self, st, q_ap, qbufs, keys, n, out_dram_ap, out_buf):
        ps = st["ps"]
        st["o"] += 1
        po = ps[st["o"] % 2]
        nk = len(keys)
        for j, (kT, kbufs, va, vbufs, bias, bbufs) in enumerate(keys):
            st["s"] = (st["s"] + 1) % 6
            pS = ps[2 + st["s"]]
            self.mm(pS.t[:, 0:n], kT, q_ap, start=True, stop=(bias is None), r=list(kbufs) + list(qbufs), w=[pS])
            if bias is not None:
                self.mm(pS.t[:, 0:n], self.ident.t[:], bias, start=False, stop=True, r=[self.ident] + list(bbufs), w=[pS])
            st["p"] = (st["p"] + 1) % len(st["P"])
            P = st["P"][st["p"]]
            self.act(P.t[:, 0:n], pS.t[:, 0:n], AF.Exp, r=[pS], w=[P])
            self.mm(po.t[0:65, 0:n], va, P.t[:, 0:n], start=(j == 0), stop=(j == nk - 1), r=list(vbufs) + [P], w=[po])
        st["f"] += 1
        osb = st["osb"][st["f"] % 2]
        self.copy("vector", osb.t[0:65, 0:n], po.t[0:65, 0:n], r=[po], w=[osb])
        st["s"] = (st["s"] + 1) % 6
        pb = ps[2 + st["s"]]
        self.mm(pb.t[0:64, 0:n], self.onesf.t[64:65, 0:64], osb.t[64:65, 0:n], r=[self.onesf, osb], w=[pb])
        rc = st["rc"][st["f"] % 2]
        self.recip(rc.t[0:64, 0:n], pb.t[0:64, 0:n], r=[pb], w=[rc])
        og = st["og"][st["f"] % 2]
        self.tt("gpsimd", og.t[0:64, 0:n], osb.t[0:64, 0:n], rc.t[0:64, 0:n], ALU.mult, r=[osb, rc], w=[og])
        self.dma("gpsimd", out_dram_ap, og.t[0:64, 0:n], r=[og], w=[out_buf])

    def attn_state(self, es):
        ps = self.psum_banks(es, 8)
        return dict(ps=ps, o=0, s=0, p=0, f=0,
                    P=[self.sb(es, "P%d" % i, [128, 512], BF16) for i in range(4)],
                    osb=[self.sb(es, "osb%d" % i, [128, 512], F32) for i in range(2)],
                    rc=[self.sb(es, "rc%d" % i, [64, 512], F32) for i in range(2)],
                    og=[self.sb(es, "og%d" % i, [64, 512], BF16) for i in range(2)])

    def stage_na(self, l, sc, last):
        I = self.inputs
        S, T, NT, NM = self.S, self.T, self.NT, self.NM
        with ExitStack() as es:
            st = self.attn_state(es)
            kc = self.sb(es, "nakc", [64, NH, CTX], BF16)
            self.dma("sync", kc.t[:], sc["nak"].t[:, :, S:S + CTX].rearrange("h d t -> d h t"), r=[sc["nak"]], w=[kc])
            vc = self.sb(es, "navc", [128, 2, NH, 65], BF16)
            self.memset("vector", vc.t[:, :, :, 64:65], 1.0, w=[vc])
            for j in range(2):
                self.dma("sync", vc.t[:, j, :, 0:64], sc["nav"].t[S + j * 128:S + (j + 1) * 128, :].rearrange("p (h d) -> p h d", d=64), r=[sc["nav"]], w=[vc])
            bt = self.sb(es, "nabt", [128, 8, NH, 512], BF16)
            kws = [self.sb(es, "nakw%d" % i, [64, NH, 8 * 128], BF16) for i in range(2)]
            vws = [self.sb(es, "navw%d" % i, [128, 8, NH, 65], BF16) for i in range(2)]
            for v in vws:
                self.memset("vector", v.t[:, :, :, 64:65], 1.0, w=[v])
            qs = [self.sb(es, "naq%d" % i, [64, NH, 512], BF16) for i in range(2)]
            loaded = None
            for m in range(NM):
                kt0, kt1 = max(0, 4 * m - 2), min(NT - 1, 4 * m + 5)
                nk = kt1 - kt0 + 1
                if m == 0:
                    kind, base, d0 = "first", 8, 0
                elif m == NM - 1:
                    kind, base, d0 = "last", 14, -2
                else:
                    kind, base, d0 = "int", 0, -2
                if loaded != kind:
                    nsl = 8 if kind == "int" else 6
                    R = self.R
                    for sl in range(nsl):
                        for kr in range(2):
                            for qr in range(8):
                                if kind == "int":
                                    dr = 2 * (sl + d0) + kr - qr
                                    blk = dr + 7 if -4 <= dr <= 3 else 15
                                else:
                                    key_row = 2 * (4 * m + sl + d0) + kr
                                    r = 8 * m + qr
                                    r0 = min(max(r - 4, 0), R - 8)
                                    blk = (key_row - r + 7) if (r0 <= key_row < r0 + 8) else 15
                                self.dma("gpsimd", bt.t[kr * 64:(kr + 1) * 64, sl, :, qr * 64:(qr + 1) * 64],
                                         I["na_blk"].t[l, :, blk].rearrange("h k q -> k h q"), r=[I["na_blk"]], w=[bt])
                    loaded = kind
                kw, vw, q = kws[m % 2], vws[m % 2], qs[m % 2]
                self.dma("sync", kw.t[:, :, 0:nk * 128], sc["nak"].t[:, :, kt0 * 128:(kt1 + 1) * 128].rearrange("h d t -> d h t"), r=[sc["nak"]], w=[kw])
                for j in range(nk):
                    self.dma("sync", vw.t[:, j, :, 0:64], sc["nav"].t[(kt0 + j) * 128:(kt0 + j + 1) * 128, :].rearrange("p (h d) -> p h d", d=64),
                             r=[sc["nav"]], w=[vw])
                self.dma("sync", q.t[:], sc["naq"].t[:, :, m * 512:(m + 1) * 512].rearrange("h d t -> d h t"), r=[sc["naq"]], w=[q])
                for h in range(NH):
                    keys = []
                    for kt in range(kt0, kt1 + 1):
                        sl = (kt - 4 * m) - d0
                        keys.append((kw.t[:, h, (kt - kt0) * 128:(kt - kt0 + 1) * 128], [kw], vw.t[:, kt - kt0, h, :], [vw], bt.t[:, sl, h, :], [bt]))
                    for j in range(2):
                        keys.append((kc.t[:, h, j * 128:(j + 1) * 128], [kc], vc.t[:, j, h, :], [vc], None, []))
                    self.attn(st, q.t[:, h, :], [q], keys, 512, sc["ona"].t[h * 64:(h + 1) * 64, m * 512:(m + 1) * 512], sc["ona"])
            if not last:
                q = qs[NM % 2]
                self.dma("sync", q.t[:, :, 0:CTX], sc["naq"].t[:, :, S:S + CTX].rearrange("h d t -> d h t"), r=[sc["naq"]], w=[q])
                for h in range(NH):
                    keys = [(kc.t[:, h, j * 128:(j + 1) * 128], [kc], vc.t[:, j, h, :], [vc], None, []) for j in range(2)]
                    self.attn(st, q.t[:, h, 0:CTX], [q], keys, CTX, sc["ona"].t[h * 64:(h + 1) * 64, S:S + CTX], sc["ona"])

    def stage_mla(self, l, sc, last):
        S, T = self.S, self.T
        NTT = T // 128
        with ExitStack() as es:
            st = self.attn_state(es)
            Ks = [self.sb(es, "mK%d" % i, [96, T], BF16) for i in range(2)]
            Qs = [self.sb(es, "mQ%d" % i, [96, T], BF16) for i in range(2)]
            Vs = [self.sb(es, "mV%d" % i, [128, NTT, 65], BF16) for i in range(2)]
            for v in Vs:
                self.memset("vector", v.t[:, :, 64:65], 1.0, w=[v])
            for h in range(NH):
                K, Q, V = Ks[h % 2], Qs[h % 2], Vs[h % 2]
                self.dma("sync", K.t[:], sc["km"].t[h], r=[sc["km"]], w=[K])
                nq = S if last else T
                self.dma("sync", Q.t[:, 0:nq], sc["qm"].t[h, :, 0:nq], r=[sc["qm"]], w=[Q])
                self.dma("sync", V.t[:, :, 0:64], sc["vm"].t[h], r=[sc["vm"]], w=[V])
                allk = [(K.t[:, j * 128:(j + 1) * 128], [K], V.t[:, j, :], [V], None, []) for j in range(NTT)]
                for (c0, n) in self.macro_tiles(0, S):
                    self.attn(st, Q.t[:, c0:c0 + n], [Q], allk, n, sc["omla"].t[h * 64:(h + 1) * 64, c0:c0 + n], sc["omla"])
                if not last:
                    self.attn(st, Q.t[:, S:T], [Q], allk[S // 128:], CTX, sc["omla"].t[h * 64:(h + 1) * 64, S:T], sc["omla"])

    def post_norm_residual(self, ybanks, xres, G, xo, tmp):
        ss2, rstd, junk, t1 = tmp["ss2"], tmp["rstd"], tmp["junkf"], tmp["t1"]
        for hf in range(2):
            self.act(junk.t[:, 0:512], ybanks[hf].t[:], AF.Square, r=[ybanks[hf]], w=[junk, ss2], accum_out=ss2.t[:, hf:hf + 1])
        self.tt("vector", ss2.t[:, 2:3], ss2.t[:, 0:1], ss2.t[:, 1:2], ALU.add, r=[ss2], w=[ss2])
        self.act(rstd.t[:, 0:1], ss2.t[:, 2:3], AF.Sqrt, r=[ss2], w=[rstd], bias=EPS, scale=1.0 / D)
        self.recip(rstd.t[:, 0:1], rstd.t[:, 0:1], r=[rstd], w=[rstd])
        for hf in range(2):
            sl = slice(hf * 512, (hf + 1) * 512)
            self.stt("vector", t1.t[:, sl], ybanks[hf].t[:], rstd.t[:, 0:1], G.t[:, sl], ALU.mult, ALU.mult, r=[ybanks[hf], rstd, G], w=[t1])
        self.tt("gpsimd", xo.t[:], t1.t[:], xres.t[:], ALU.add, r=[t1, xres], w=[xo])

    def stage_merge(self, l, mod, xsrc, csrc, sc, last):
        I = self.inputs
        S, T = self.S, self.T
        with ExitStack() as es:
            ps = self.psum_banks(es, 8)
            pT = ps[7]
            ybanks = ps[5:7]
            wg = self.sb(es, "wg", [128, 8, 3072], BF16)
            for k in range(8):
                self.dma("gpsimd", wg.t[:, k, :], I["w_in"].t[l][k * 128:(k + 1) * 128, 2848:IN_COLS], r=[I["w_in"]], w=[wg])
            wbr = self.sb(es, "wbr", [128, 12, D], BF16)
            self.dma("gpsimd", wbr.t[:], I["w_branch"].t[l].rearrange("b (c p) n -> p (b c) n", p=128), r=[I["w_branch"]], w=[wbr])
            wo = self.sb(es, "wo", [128, 8, D], BF16)
            self.dma("gpsimd", wo.t[:], I["w_o"].t[l].rearrange("(c p) n -> p c n", p=128), r=[I["w_o"]], w=[wo])
            hTs = [self.sb(es, "mhT%d" % i, [128, 8, 512], BF16) for i in range(2)]
            brs = [[self.sb(es, "mbr%d_%d" % (b, i), [128, 4, 512], BF16) for b in range(3)] for i in range(2)]
            sig = [self.sb(es, "sig%d" % i, [128, 512], F32) for i in range(3)]
            ta = [self.sb(es, "mta%d" % i, [128, 512], F32) for i in range(3)]
            mg = self.sb(es, "mg", [128, 8, 512], BF16)
            xts = [self.sb(es, "mxt%d" % i, [128, D], F32) for i in range(2)]
            xos = [self.sb(es, "mxo%d" % i, [128, D], F32) for i in range(2)]
            tmp = dict(ss2=self.sb(es, "mss2", [128, 3], F32), rstd=self.sb(es, "mrstd", [128, 1], F32),
                       junkf=self.sb(es, "mjunkf", [128, 512], BF16), t1=self.sb(es, "mt1", [128, D], F32))
            tmp2 = dict(junk=self.sb(es, "mjunk", [128, D], BF16), ss=self.sb(es, "mss", [128, 1], F32),
                        rstd=self.sb(es, "mrstd2", [128, 1], F32), xn=self.sb(es, "mxn", [128, D], BF16))
            h2s = [self.sb(es, "mh2%d" % i, [128, 8, 128], BF16) for i in range(2)]
            srcs = [("ona", 0), ("opool", 1), ("omla", 2)]
            mi = 0
            ti = 0
            for (w, s0, slen) in self.segs(last):
                src = xsrc if w == 0 else csrc
                for (c0, n) in self.macro_tiles(s0, slen):
                    mi += 1
                    hT = hTs[mi % 2]
                    br = brs[mi % 2]
                    self.dma("sync", hT.t[:, :, 0:n], sc["hT"].t[:, c0:c0 + n].rearrange("(k p) t -> p k t", p=128), r=[sc["hT"]], w=[hT])
                    for (nm, b) in srcs:
                        self.dma("sync", br[b].t[:, :, 0:n], sc[nm].t[:, c0:c0 + n].rearrange("(k p) t -> p k t", p=128), r=[sc[nm]], w=[br[b]])
                    for c in range(8):
                        for b in range(3):
                            pg = ps[b % 2]
                            for k in range(8):
                                self.mm(pg.t[:, 0:n], wg.t[:, k, b * D + c * 128:b * D + (c + 1) * 128], hT.t[:, k, 0:n], start=(k == 0), stop=(k == 7),
                                        r=[wg, hT], w=[pg])
                            self.act(sig[b].t[:, 0:n], pg.t[:, 0:n], AF.Sigmoid, r=[pg], w=[sig[b]])
                            pp = ps[2 + b]
                            for k in range(4):
                                self.mm(pp.t[:, 0:n], wbr.t[:, b * 4 + k, c * 128:(c + 1) * 128], br[b].t[:, k, 0:n], start=(k == 0), stop=(k == 3),
                                        r=[wbr, br[b]], w=[pp])
                            self.tt("vector", ta[b].t[:, 0:n], pp.t[:, 0:n], sig[b].t[:, 0:n], ALU.mult, r=[pp, sig[b]], w=[ta[b]])
                        self.tt("gpsimd", ta[0].t[:, 0:n], ta[0].t[:, 0:n], ta[1].t[:, 0:n], ALU.add, r=[ta[0], ta[1]], w=[ta[0]])
                        self.tt("gpsimd", mg.t[:, c, 0:n], ta[0].t[:, 0:n], ta[2].t[:, 0:n], ALU.add, r=[ta[0], ta[2]], w=[mg])
                    for i in range(n // 128):
                        ti += 1
                        xt, xo = xts[ti % 2], xos[ti % 2]
                        r0 = c0 - s0 + i * 128
                        self.dma("sync", xt.t[:], src.t[r0:r0 + 128, :], r=[src], w=[xt])
                        for hf in range(2):
                            for c in range(8):
                                self.mm(ybanks[hf].t[:], mg.t[:, c, i * 128:(i + 1) * 128], wo.t[:, c, hf * 512:(hf + 1) * 512], start=(c == 0), stop=(c == 7),
                                        r=[mg, wo], w=[ybanks[hf]])
                        self.post_norm_residual(ybanks, xt, mod["G1"][w], xo, tmp)
                        self.dma("gpsimd", sc["xmid"].t[c0 + i * 128:c0 + (i + 1) * 128, :], xo.t[:], r=[xo], w=[sc["xmid"]])
                        h2 = h2s[ti % 2]
                        self.norm_transpose(xo, mod["A2"], 24, mod["modT"], w, h2, 0, tmp2, pT)
                        self.dma("gpsimd", sc["h2T"].t[:, c0 + i * 128:c0 + (i + 1) * 128].rearrange("(k p) t -> p k t", p=128), h2.t[:], r=[h2], w=[sc["h2T"]])

    def stage_ffn(self, l, mod, sc, xo_d, xco_d, last):
        I = self.inputs
        S, T = self.S, self.T
        with ExitStack() as es:
            ps = self.psum_banks(es, 8)
            ybanks = ps[0:2]
            rot = [0]

            def nb():
                rot[0] = (rot[0] + 1) % 6
                return ps[2 + rot[0]]

            wd = self.sb(es, "wd", [128, 22, D], BF16)
            self.dma("gpsimd", wd.t[:], I["w_down"].t[l].rearrange("(c p) n -> p c n", p=128), r=[I["w_down"]], w=[wd])
            cw = self.sb(es, "cw", [128, 44, 3], F32)
            self.dma("sync", cw.t[:], I["conv_wT"].t[l], r=[I["conv_wT"]], w=[cw])
            cb = self.sb(es, "cb", [128, 44], F32)
            self.dma("sync", cb.t[:], I["conv_bT"].t[l], r=[I["conv_bT"]], w=[cb])
            NB = 1024
            h2s = [self.sb(es, "fh2_%d" % i, [128, 8, NB + 2], BF16) for i in range(2)]
            wus = [self.sb(es, "fwu%d" % i, [128, 8, 256], BF16) for i in range(3)]
            us = [self.sb(es, "fu%d" % i, [128, NB + 2], F32) for i in range(2)]
            accs = [self.sb(es, "facc%d" % i, [128, NB], F32) for i in range(2)]
            ga = self.sb(es, "fga", [128, NB], F32)
            actT = self.sb(es, "factT", [128, 22, NB], BF16)
            xts = [self.sb(es, "fxt%d" % i, [128, D], F32) for i in range(2)]
            xos = [self.sb(es, "fxo%d" % i, [128, D], F32) for i in range(2)]
            tmp = dict(ss2=self.sb(es, "fss2", [128, 3], F32), rstd=self.sb(es, "frstd", [128, 1], F32),
                       junkf=self.sb(es, "fjunkf", [128, 512], BF16), t1=self.sb(es, "ft1", [128, D], F32))
            bi = 0
            wi = 0
            ti = 0
            ev = 0
            for (w, s0, slen) in self.segs(last):
                o = 0
                while o < slen:
                    nbk = min(NB, slen - o)
                    b0 = s0 + o
                    bi += 1
                    h2 = h2s[bi % 2]
                    lo = max(b0 - 1, s0)
                    hi = min(b0 + nbk + 1, s0 + slen)
                    if lo == b0:
                        self.memset("vector", h2.t[:, :, 0:1], 0.0, w=[h2])
                    if hi == b0 + nbk:
                        self.memset("vector", h2.t[:, :, nbk + 1:nbk + 2], 0.0, w=[h2])
                    self.dma("sync", h2.t[:, :, lo - (b0 - 1):hi - (b0 - 1)], sc["h2T"].t[:, lo:hi].rearrange("(k p) t -> p k t", p=128),
                             r=[sc["h2T"]], w=[h2])
                    ncols = nbk + 2
                    pieces = []
                    pp = 0
                    npc = (ncols + 511) // 512
                    psz = (ncols + npc - 1) // npc
                    while pp < ncols:
                        pieces.append((pp, min(psz, ncols - pp)))
                        pp += psz
                    for i in range(22):
                        wi += 1
                        wu = wus[wi % 3]
                        self.dma("gpsimd", wu.t[:, :, 0:128], I["w_up"].t[l][:, i * 128:(i + 1) * 128].rearrange("(k p) n -> p k n", p=128),
                                 r=[I["w_up"]], w=[wu])
                        self.dma("gpsimd", wu.t[:, :, 128:256], I["w_up"].t[l][:, DFF + i * 128:DFF + (i + 1) * 128].rearrange("(k p) n -> p k n", p=128),
                                 r=[I["w_up"]], w=[wu])
                        for ab in range(2):
                            u = us[ab]
                            for (p0, pn) in pieces:
                                pb = nb()
                                for k in range(8):
                                    self.mm(pb.t[:, 0:pn], wu.t[:, k, ab * 128:(ab + 1) * 128], h2.t[:, k, p0:p0 + pn], start=(k == 0), stop=(k == 7),
                                            r=[wu, h2], w=[pb])
                                ev += 1
                                self.copy("scalar" if ev % 2 else "vector", u.t[:, p0:p0 + pn], pb.t[:, 0:pn], r=[pb], w=[u])
                            ch = ab * 22 + i
                            acc = accs[ab]
                            self.ts("gpsimd", acc.t[:, 0:nbk], u.t[:, 0:nbk], cw.t[:, ch, 0:1], cb.t[:, ch:ch + 1], ALU.mult, ALU.add,
                                    r=[u, cw, cb], w=[acc])
                            self.stt("vector", acc.t[:, 0:nbk], u.t[:, 1:nbk + 1], cw.t[:, ch, 1:2], acc.t[:, 0:nbk], ALU.mult, ALU.add,
                                     r=[u, cw, acc], w=[acc])
                            self.stt("vector", acc.t[:, 0:nbk], u.t[:, 2:nbk + 2], cw.t[:, ch, 2:3], acc.t[:, 0:nbk], ALU.mult, ALU.add,
                                     r=[u, cw, acc], w=[acc])
                        self.act(ga.t[:, 0:nbk], accs[0].t[:, 0:nbk], AF.Gelu_apprx_tanh, r=[accs[0]], w=[ga])
                        self.tt("vector", actT.t[:, i, 0:nbk], ga.t[:, 0:nbk], accs[1].t[:, 0:nbk], ALU.mult, r=[ga, accs[1]], w=[actT])
                    for tt_i in range(nbk // 128):
                        ti += 1
                        xt, xo = xts[ti % 2], xos[ti % 2]
                        row = b0 + tt_i * 128
                        self.dma("sync", xt.t[:], sc["xmid"].t[row:row + 128, :], r=[sc["xmid"]], w=[xt])
                        for hf in range(2):
                            for i in range(22):
                                self.mm(ybanks[hf].t[:], actT.t[:, i, tt_i * 128:(tt_i + 1) * 128], wd.t[:, i, hf * 512:(hf + 1) * 512],
                                        start=(i == 0), stop=(i == 21), r=[actT, wd], w=[ybanks[hf]])
                        self.post_norm_residual(ybanks, xt, mod["G2"][w], xo, tmp)
                        if w == 0:
                            self.dma("gpsimd", xo_d.t[row:row + 128, :], xo.t[:], r=[xo], w=[xo_d])
                        else:
                            self.dma("gpsimd", xco_d.t[row - S:row - S + 128, :], xo.t[:], r=[xo], w=[xco_d])
                    o += nbk


def rope_tables(S):
    n_freq = 8
    inv = (10000.0 ** (-np.arange(n_freq, dtype=np.float32) / n_freq)).astype(np.float32)
    t = np.arange(S)
    row = (t // 64).astype(np.float32)
    col = (t % 64).astype(np.float32)
    ar = row[:, None] * inv[None, :]
    ac = col[:, None] * inv[None, :]
    cos = np.concatenate([np.cos(ar), np.cos(ar), np.cos(ac), np.cos(ac)], axis=1).T
    sin = np.concatenate([-np.sin(ar), np.sin(ar), -np.sin(ac), np.sin(ac)], axis=1).T
    k = np.stack([cos, sin]).astype(np.float32)
    q = (k * np.float32(96 ** -0.5)).astype(np.float32)
    return np.ascontiguousarray(q), np.ascontiguousarray(k)


ROPE_PERM = np.array(list(range(8, 16)) + list(range(0, 8)) + list(range(24, 32)) + list(range(16, 24)))


def pool_inv_table(S):
    out = np.zeros((4, 1, S + CTX), np.float32)
    for gi, win in enumerate((2, 4, 8, 16)):
        for (s0, L) in ((0, S), (S, CTX)):
            t = np.arange(L)
            lo = np.clip(t - win // 2, 0, L)
            hi = np.clip(t + win // 2, 0, L)
            out[gi, 0, s0:s0 + L] = 1.0 / (hi - lo).astype(np.float32)
    return out


def na_bias_blocks(rpb):
    kcol = np.arange(64)
    qcol = np.arange(64)
    win0 = np.clip(qcol - 8, 0, 48)
    in_col = (kcol[:, None] >= win0[None, :]) & (kcol[:, None] < win0[None, :] + 16)
    rel_c = np.clip(kcol[:, None] - qcol[None, :] + 15, 0, 30)
    out = np.full((8, 16, 64, 64), NEG, np.float32)
    for dr in range(15):
        out[:, dr] = np.where(in_col[None], rpb[:, dr][:, rel_c], np.float32(NEG))
    return out


def host_inputs(inp, b, S, L):
    f = np.float32
    g = {}
    g["x"] = np.ascontiguousarray(inp["x"][b, :S])
    g["ctx"] = np.ascontiguousarray(inp["ctx"][b])
    cc = np.stack([inp["c"][b], inp["c_ctx"]], axis=0)
    g["ccT"] = np.ascontiguousarray(cc.reshape(2, 8, 128).transpose(2, 1, 0))
    g["w_ada"] = np.ascontiguousarray(inp["w_ada"][:L])
    g["b_adaT"] = np.ascontiguousarray(inp["b_ada"][:L].reshape(L, 48, 128).transpose(0, 2, 1))
    g["b_ada"] = np.ascontiguousarray(inp["b_ada"][:L].reshape(L, 1, 6 * D))
    norms = np.stack([inp["norm_pre1"][:L], inp["norm_post1"][:L], inp["norm_pre2"][:L], inp["norm_post2"][:L]], axis=1)
    g["normsT"] = np.ascontiguousarray(norms.reshape(L, 4, 8, 128).transpose(0, 1, 3, 2))
    g["norms"] = np.ascontiguousarray(norms.reshape(L, 4, 1, D))
    w_in = inp["w_in"][:L]
    g["w_in"] = np.ascontiguousarray(w_in)
    kr = w_in[:, :, 2816:2848]
    dummy = w_in[:, :, 0:64]
    g["w_kr96"] = np.ascontiguousarray(np.concatenate([dummy, kr, dummy, kr[:, :, ROPE_PERM]], axis=2))
    wuq = inp["w_uq"][:L].reshape(L, 512, 8, 96)
    wuq_p = np.concatenate([wuq[..., :64], wuq[..., 64:][..., ROPE_PERM]], axis=-1)
    g["w_uq2"] = np.ascontiguousarray(np.concatenate([wuq.reshape(L, 512, 768), wuq_p.reshape(L, 512, 768)], axis=2))
    wukv = inp["w_ukv"][:L].reshape(L, 256, 8, 128)
    g["w_ukv_k"] = np.ascontiguousarray(wukv[..., :64].reshape(L, 256, 512))
    g["w_ukv_v"] = np.ascontiguousarray(wukv[..., 64:].reshape(L, 256, 512))
    g["qnT"] = np.ascontiguousarray(inp["mla_q_norm"][:L].reshape(L, 4, 128).transpose(0, 2, 1))
    g["kvnT"] = np.ascontiguousarray(inp["mla_kv_norm"][:L].reshape(L, 2, 128).transpose(0, 2, 1))
    g["pool_w"] = np.ascontiguousarray(inp["pool_w"][:L])
    g["pool_scaleT"] = np.ascontiguousarray(inp["pool_scale"][:L].reshape(L, 4, 128).transpose(0, 2, 1))
    g["w_branch"] = np.ascontiguousarray(inp["w_branch"][:L])
    g["w_o"] = np.ascontiguousarray(inp["w_o"][:L])
    g["w_up"] = np.ascontiguousarray(inp["w_up"][:L])
    g["conv_wT"] = np.ascontiguousarray(inp["conv_w"][:L].reshape(L, 3, 44, 128).transpose(0, 3, 2, 1))
    g["conv_bT"] = np.ascontiguousarray(inp["conv_b"][:L].reshape(L, 44, 128).transpose(0, 2, 1))
    g["w_down"] = np.ascontiguousarray(inp["w_down"][:L])
    g["na_blk"] = np.stack([na_bias_blocks(np.asarray(inp["na_rpb"][l], f)) for l in range(L)])
    q, k = rope_tables(S)
    g["ropeq"], g["ropek"] = q, k
    g["pool_inv"] = pool_inv_table(S)
    g["ident"] = np.eye(128, dtype=f)
    sel = np.zeros((2, 256), f)
    sel[0, :128] = 1.0
    sel[1, 128:] = 1.0
    g["sel"] = sel
    return {k2: np.asarray(v, f) for k2, v in g.items()}


_CACHE = {}


def run(inp, S, L, n_batch, debug=(), cores_per_batch=2, stop_after=None):
    kb = KB(S, L, debug)
    kb.stop_after = stop_after
    nc = kb.build()
    in_maps = []
    shared = None
    for b in range(n_batch):
        hm = host_inputs(inp, b, S, L)
        if shared is None:
            shared = hm
        else:
            for k in hm:
                if k not in ("x", "ctx", "ccT"):
                    hm[k] = shared[k]
        for _ in range(cores_per_batch):
            in_maps.append(hm)
    import time as _t
    _t0 = _t.time()
    res = run_bass_kernel_spmd(nc, in_maps, core_ids=list(range(len(in_maps))))
    print("[kernel] device run (compile+transfer+exec) %.1fs" % (_t.time() - _t0), flush=True)
    return kb, res


def kernel(**inputs):
    inp = {k: np.asarray(v) for k, v in inputs.items()}
    B = inp["x"].shape[0]
    S = inp["x"].shape[1]
    L = inp["w_ada"].shape[0]
    kb, res = run(inp, S, L, B, cores_per_batch=1)
    out = np.stack([res.results[b]["out"] for b in range(B)], axis=0)
    return out.astype(np.float32)
```

```python
import numpy as np
import concourse.bass as bass
import concourse.mybir as mybir
from concourse.bass_utils import run_bass_kernel_spmd
from contextlib import ExitStack

F32 = mybir.dt.float32
BF16 = mybir.dt.bfloat16
AF = mybir.ActivationFunctionType
ALU = mybir.AluOpType

SAME_ENG_SYNC = True
NDMA_SEMS = 8

D = 1024
NH = 8
CTX = 256
DFF = 2816
IN_COLS = 5920
EPS = 1e-6
NEG = -30000.0


class Buf:
    __slots__ = ("name", "multi", "excl", "w_eng", "w_dma", "r_eng", "r_dma")

    def __init__(self, name="", multi=False):
        self.name = name
        self.multi = multi
        self.excl = False
        self.w_eng = {}
        self.w_dma = []
        self.r_eng = {}
        self.r_dma = []


class TT:
    __slots__ = ("t", "b")

    def __init__(self, t, name="", multi=False):
        self.t = t
        self.b = Buf(name, multi)


class Ins:
    __slots__ = ("eng", "fn", "deps", "marked", "sem", "val", "is_dma", "idx", "thr")


def _b(x):
    return x.b if isinstance(x, TT) else x


class Prog:
    ENGS = ("tensor", "vector", "scalar", "gpsimd", "sync")

    def __init__(self, nc):
        self.nc = nc
        self.streams = {e: [] for e in self.ENGS}
        self.n = 0
        self.bar_pending = {}
        self.dma_since = []
        self.last_c = {}

    def barrier(self):
        deps = list(self.last_c.values()) + list(self.dma_since)
        self.dma_since = []
        for e in self.ENGS:
            self.bar_pending[e] = list(self.bar_pending.get(e, [])) + deps

    def op(self, eng, fn, reads=(), writes=(), dma=False):
        ins = Ins()
        ins.eng = eng
        ins.fn = fn
        ins.is_dma = dma
        ins.marked = False
        ins.idx = self.n
        ins.sem = None
        ins.val = 0
        ins.thr = None
        self.n += 1
        deps = {}

        def add(d):
            if not d.is_dma:
                if d.eng == eng and not dma and (eng == "tensor" or not SAME_ENG_SYNC):
                    return
                k = ("c", d.eng)
                if k not in deps or deps[k].idx < d.idx:
                    deps[k] = d
            else:
                deps[("d", d.idx)] = d

        reads = [_b(x) for x in reads]
        writes = [_b(x) for x in writes]
        if self.bar_pending.get(eng):
            for d in self.bar_pending[eng]:
                add(d)
            self.bar_pending[eng] = []
        if dma:
            self.dma_since.append(ins)
        else:
            self.last_c[eng] = ins
        for b in reads:
            for d in b.w_eng.values():
                add(d)
            for d in b.w_dma:
                add(d)
            if b.excl:
                for e2, d in b.r_eng.items():
                    if e2 != eng:
                        add(d)
        for b in writes:
            if not b.multi:
                for d in b.w_eng.values():
                    add(d)
                for d in b.w_dma:
                    add(d)
            for d in b.r_eng.values():
                add(d)
            for d in b.r_dma:
                add(d)
        ins.deps = list(deps.values())
        for d in ins.deps:
            d.marked = True
        for b in reads:
            if dma:
                b.r_dma.append(ins)
            else:
                b.r_eng[eng] = ins
        for b in writes:
            if not b.multi:
                b.w_eng = {}
                b.w_dma = []
                b.r_eng = {}
                b.r_dma = []
            if dma:
                b.w_dma.append(ins)
            else:
                b.w_eng[eng] = ins
        self.streams[eng].append(ins)
        return ins

    def emit(self, final_wait_eng="sync"):
        nc = self.nc
        with ExitStack() as es:
            tick = {}
            for e in ("tensor", "vector", "scalar", "gpsimd"):
                tick[e] = es.enter_context(nc.semaphore("tk_" + e))
            dsem = {}
            for e in ("sync", "gpsimd", "scalar"):
                dsem[e] = [es.enter_context(nc.semaphore("d_%s%d" % (e, i))) for i in range(NDMA_SEMS)]
            all_dma = []
            for e, st in self.streams.items():
                cnt = 0
                nd = 0
                lastslot = [None] * NDMA_SEMS
                for ins in st:
                    if ins.is_dma:
                        slot = nd % NDMA_SEMS
                        prev = lastslot[slot]
                        ins.sem = dsem[e][slot]
                        ins.val = (prev.val if prev is not None else 0) + 16
                        ins.thr = prev
                        lastslot[slot] = ins
                        nd += 1
                        all_dma.append(ins)
                    elif ins.marked:
                        cnt += 1
                        ins.sem = tick[e]
                        ins.val = cnt
            block = es.enter_context(nc.Block())
            counts = {}

            def run_stream(ename, eobj):
                waited = {}
                n = 0

                def w(sem, val):
                    k = id(sem)
                    if waited.get(k, 0) >= val:
                        return 0
                    waited[k] = val
                    eobj.wait_ge(sem, val)
                    return 1

                for ins in self.streams[ename]:
                    if ins.thr is not None:
                        n += w(ins.thr.sem, ins.thr.val)
                    for d in ins.deps:
                        n += w(d.sem, d.val)
                    bi = ins.fn(eobj)
                    n += 1
                    if ins.is_dma:
                        bi.then_inc(ins.sem, 16)
                    elif ins.marked:
                        bi.then_inc(ins.sem, 1)
                if ename == final_wait_eng:
                    for d in reversed(all_dma):
                        n += w(d.sem, d.val)
                counts[ename] = n

            @block.sync
            def _(e):
                run_stream("sync", e)

            @block.tensor
            def _(e):
                run_stream("tensor", e)

            @block.vector
            def _(e):
                run_stream("vector", e)

            @block.scalar
            def _(e):
                run_stream("scalar", e)

            @block.gpsimd
            def _(e):
                run_stream("gpsimd", e)

            self.counts = counts


class KB:
    def __init__(self, S, L, debug=()):
        self.S = S
        self.L = L
        self.T = S + CTX
        self.NT = S // 128
        self.NM = S // 512
        self.R = S // 64
        self.debug = set(debug)
        self.nc = bass.Bass("TRN2", target_bir_lowering=False)
        self.p = Prog(self.nc)
        self.inputs = {}
        self.uid = 0
        self.stop_after = None

    def inp(self, name, shape, dt=F32):
        t = self.nc.dram_tensor(name, list(shape), dt, kind="ExternalInput").ap()
        tt = TT(t, name, multi=True)
        self.inputs[name] = tt
        return tt

    def dram(self, name, shape, dt):
        kind = "ExternalOutput" if name in self.debug else "Internal"
        t = self.nc.dram_tensor(name, list(shape), dt, kind=kind).ap()
        return TT(t, name, multi=True)

    def sb(self, es, name, shape, dt):
        self.uid += 1
        t = es.enter_context(self.nc.sbuf_tensor("%s_%d" % (name, self.uid), list(shape), dt))
        return TT(t, name)

    def psum_banks(self, es, n=8):
        es.callback(self.p.barrier)
        banks = []
        for i in range(n):
            self.uid += 1
            t = es.enter_context(self.nc.psum_tensor("ps%d_%d" % (i, self.uid), [128, 512], F32))
            banks.append(TT(t, "ps%d" % i))
            banks[-1].b.excl = True
        return banks

    def mm(self, out, lhsT, rhs, start=True, stop=True, r=(), w=()):
        self.p.op("tensor", lambda e: e.matmul(out, lhsT=lhsT, rhs=rhs, start=start, stop=stop), r, w)

    def tr(self, out, in_, ident, r=(), w=()):
        self.p.op("tensor", lambda e: e.transpose(out=out, in_=in_, identity=ident), r, w)

    def act(self, out, in_, func, r=(), w=(), **kw):
        self.p.op("scalar", lambda e: e.activation(out=out, in_=in_, func=func, **kw), r, w)

    def tt(self, eng, out, in0, in1, op, r=(), w=()):
        self.p.op(eng, lambda e: e.tensor_tensor(out=out, in0=in0, in1=in1, op=op), r, w)

    def ts(self, eng, out, in0, s1, s2, op0, op1=None, r=(), w=()):
        if op1 is None:
            self.p.op(eng, lambda e: e.tensor_scalar(out=out, in0=in0, scalar1=s1, scalar2=None, op0=op0), r, w)
        else:
            self.p.op(eng, lambda e: e.tensor_scalar(out=out, in0=in0, scalar1=s1, scalar2=s2, op0=op0, op1=op1), r, w)

    def stt(self, eng, out, in0, scalar, in1, op0, op1, r=(), w=()):
        self.p.op(eng, lambda e: e.scalar_tensor_tensor(out=out, in0=in0, scalar=scalar, in1=in1, op0=op0, op1=op1), r, w)

    def copy(self, eng, out, in_, r=(), w=()):
        if eng == "scalar":
            self.p.op(eng, lambda e: e.copy(out=out, in_=in_), r, w)
        else:
            self.p.op(eng, lambda e: e.tensor_copy(out=out, in_=in_), r, w)

    def recip(self, out, in_, r=(), w=()):
        self.p.op("vector", lambda e: e.reciprocal(out=out, in_=in_), r, w)

    def memset(self, eng, ap, val, w=()):
        self.p.op(eng, lambda e: e.memset(ap, val), (), w)

    def dma(self, eng, out, in_, r=(), w=()):
        self.p.op(eng, lambda e: e.dma_start(out=out, in_=in_), r, w, dma=True)

    def declare_inputs(self):
        S, L, T = self.S, self.L, self.T
        i = self.inp
        i("x", [S, D]); i("ctx", [CTX, D]); i("ccT", [128, 8, 2])
        i("w_ada", [L, D, 6 * D]); i("b_adaT", [L, 128, 48]); i("b_ada", [L, 1, 6 * D])
        i("normsT", [L, 4, 128, 8]); i("norms", [L, 4, 1, D])
        i("w_in", [L, D, IN_COLS]); i("w_kr96", [L, D, 192])
        i("w_uq2", [L, 512, 1536]); i("w_ukv_k", [L, 256, 512]); i("w_ukv_v", [L, 256, 512])
        i("qnT", [L, 128, 4]); i("kvnT", [L, 128, 2])
        i("pool_w", [L, 4, 128, 128]); i("pool_scaleT", [L, 128, 4])
        i("w_branch", [L, 3, 512, D]); i("w_o", [L, D, D]); i("w_up", [L, D, 2 * DFF])
        i("conv_wT", [L, 128, 44, 3]); i("conv_bT", [L, 128, 44]); i("w_down", [L, DFF, D])
        i("na_blk", [L, NH, 16, 64, 64])
        i("ropeq", [2, 32, S]); i("ropek", [2, 32, S]); i("pool_inv", [4, 1, T])
        i("ident", [128, 128]); i("sel", [2, 256])
        out = self.nc.dram_tensor("out", [S, D], F32, kind="ExternalOutput").ap()
        self.out = TT(out, "out", multi=True)

    def mod_dbg(self, mod, l):
        if "moddbg_%d" % l not in self.debug:
            return
        d = self.dram("moddbg_%d" % l, [128, 96 + 32 + 4 * D], F32)
        self.dma("gpsimd", d.t[:, 0:96], mod["modT"].t[:].rearrange("p c w -> p (c w)"), r=[mod["modT"]], w=[d])
        self.dma("gpsimd", d.t[:, 96:112], mod["A1"].t[:].rearrange("p c w -> p (c w)"), r=[mod["A1"]], w=[d])
        self.dma("gpsimd", d.t[:, 112:128], mod["A2"].t[:].rearrange("p c w -> p (c w)"), r=[mod["A2"]], w=[d])
        for i, g in enumerate(mod["G1"] + mod["G2"]):
            self.dma("gpsimd", d.t[:, 128 + i * D:128 + (i + 1) * D], g.t[:], r=[g], w=[d])

    def segs(self, last):
        s = [(0, 0, self.S)]
        if not last:
            s.append((1, self.S, CTX))
        return s

    def macro_tiles(self, start, length):
        out = []
        o = 0
        while o < length:
            n = min(512, length - o)
            out.append((start + o, n))
            o += n
        return out

    def build(self):
        self.declare_inputs()
        I = self.inputs
        S, T, L = self.S, self.T, self.L
        xsrc, csrc = I["x"], I["ctx"]
        with ExitStack() as gs:
            self.ident = self.sb(gs, "ident", [128, 128], BF16)
            self.dma("gpsimd", self.ident.t[:], I["ident"].t, r=[I["ident"]], w=[self.ident])
            self.ones = self.sb(gs, "ones", [128, 128], BF16)
            self.memset("vector", self.ones.t[:], 1.0, w=[self.ones])
            self.onesf = self.sb(gs, "onesf", [128, 64], F32)
            self.memset("vector", self.onesf.t[:], 1.0, w=[self.onesf])
            for l in range(L):
                last = (l == L - 1)
                sc = {}
                for nm, shp, dt in [("hT", [D, T], BF16), ("naq", [NH, 64, T], BF16), ("nak", [NH, 64, T], BF16),
                                    ("nav", [T, 512], BF16), ("poolu", [512, T], F32), ("qm", [NH, 96, T], BF16),
                                    ("km", [NH, 96, T], BF16), ("vm", [NH, 128, T // 128, 64], BF16),
                                    ("ona", [512, T], BF16), ("opool", [512, T], BF16), ("omla", [512, T], BF16),
                                    ("xmid", [T, D], F32), ("h2T", [D, T], BF16), ("x1", [S, D], F32),
                                    ("xc1", [CTX, D], F32)]:
                    sc[nm] = self.dram("%s_%d" % (nm, l), shp, dt)
                with ExitStack() as ls:
                    stop = self.stop_after
                    mod = self.stage_ada(ls, l)
                    self.mod_dbg(mod, l)
                    if stop != "ada":
                        self.stage_proj(l, mod, xsrc, csrc, sc, last)
                    if stop not in ("ada", "proj"):
                        self.stage_pool(l, sc, last)
                    if stop not in ("ada", "proj", "pool"):
                        self.stage_na(l, sc, last)
                    if stop not in ("ada", "proj", "pool", "na"):
                        self.stage_mla(l, sc, last)
                    xo = self.out if last else sc["x1"]
                    if stop not in ("ada", "proj", "pool", "na", "mla"):
                        self.stage_merge(l, mod, xsrc, csrc, sc, last)
                    if stop not in ("ada", "proj", "pool", "na", "mla", "merge"):
                        self.stage_ffn(l, mod, sc, xo, sc["xc1"], last)
                xsrc, csrc = sc["x1"], sc["xc1"]
            self.p.emit()
        return self.nc

    def stage_ada(self, ls, l):
        I = self.inputs
        mod = {}
        modT = self.sb(ls, "modT", [128, 48, 2], F32)
        A1 = self.sb(ls, "A1", [128, 8, 2], F32)
        A2 = self.sb(ls, "A2", [128, 8, 2], F32)
        G1 = [self.sb(ls, "G1_%d" % w, [128, D], F32) for w in range(2)]
        G2 = [self.sb(ls, "G2_%d" % w, [128, D], F32) for w in range(2)]
        with ExitStack() as es:
            ps = self.psum_banks(es, 4)
            ccT = self.sb(es, "ccT", [128, 8, 2], F32)
            scT = self.sb(es, "scT", [128, 8, 2], F32)
            badaT = self.sb(es, "badaT", [128, 48], F32)
            nT = self.sb(es, "nT", [128, 4, 8], F32)
            brow = self.sb(es, "brow", [2, 6 * D], F32)
            modrow = self.sb(es, "modrow", [2, 6 * D], F32)
            sel = self.sb(es, "sel", [2, 256], F32)
            npost = [self.sb(es, "npost%d" % i, [128, D], F32) for i in range(2)]
            was = [self.sb(es, "wa%d" % i, [128, 8, 512], F32) for i in range(2)]
            self.dma("sync", ccT.t[:], I["ccT"].t, r=[I["ccT"]], w=[ccT])
            self.dma("sync", badaT.t[:], I["b_adaT"].t[l], r=[I["b_adaT"]], w=[badaT])
            self.dma("sync", nT.t[:], I["normsT"].t[l].rearrange("f p k -> p f k"), r=[I["normsT"]], w=[nT])
            self.dma("sync", brow.t[:], I["b_ada"].t[l].partition_broadcast(2), r=[I["b_ada"]], w=[brow])
            self.dma("sync", sel.t[:], I["sel"].t, r=[I["sel"]], w=[sel])
            self.dma("sync", npost[0].t[:], I["norms"].t[l, 1].partition_broadcast(128), r=[I["norms"]], w=[npost[0]])
            self.dma("sync", npost[1].t[:], I["norms"].t[l, 3].partition_broadcast(128), r=[I["norms"]], w=[npost[1]])
            self.act(scT.t[:], ccT.t[:], AF.Silu, r=[ccT], w=[scT])
            pcol = ps[0]
            for cb in range(12):
                wa = was[cb % 2]
                self.dma("sync", wa.t[:], I["w_ada"].t[l][:, cb * 512:(cb + 1) * 512].rearrange("(k p) n -> p k n", p=128),
                         r=[I["w_ada"]], w=[wa])
                prow = ps[1 + cb % 2]
                for k in range(8):
                    self.mm(prow.t[0:2, :], scT.t[:, k, :], wa.t[:, k, :], start=(k == 0), stop=(k == 7), r=[scT, wa], w=[prow])
                self.tt("vector", modrow.t[:, cb * 512:(cb + 1) * 512], prow.t[0:2, :], brow.t[:, cb * 512:(cb + 1) * 512], ALU.add,
                        r=[prow, brow], w=[modrow])
                for cc in range(4):
                    c = cb * 4 + cc
                    for k in range(8):
                        self.mm(pcol.t[:, 2 * c:2 * c + 2], wa.t[:, k, cc * 128:(cc + 1) * 128], scT.t[:, k, :],
                                start=(k == 0), stop=(k == 7), r=[scT, wa], w=[pcol])
            pv = pcol.t[:, 0:96].rearrange("p (c w) -> p c w", w=2)
            for w in range(2):
                self.tt("vector", modT.t[:, :, w], pv[:, :, w], badaT.t[:], ALU.add, r=[pcol, badaT], w=[modT])
            for w in range(2):
                self.stt("vector", A1.t[:, :, w], modT.t[:, 8:16, w], 1.0, nT.t[:, 0, :], ALU.add, ALU.mult, r=[modT, nT], w=[A1])
                self.stt("vector", A2.t[:, :, w], modT.t[:, 32:40, w], 1.0, nT.t[:, 2, :], ALU.add, ALU.mult, r=[modT, nT], w=[A2])
            for (G, col0, npi) in ((G1, 2 * D, 0), (G2, 5 * D, 1)):
                for w in range(2):
                    for hf in range(2):
                        pb = ps[1 + hf]
                        self.mm(pb.t[:], sel.t[0:2, w * 128:(w + 1) * 128], modrow.t[0:2, col0 + hf * 512:col0 + (hf + 1) * 512],
                                r=[sel, modrow], w=[pb])
                        self.tt("vector", G[w].t[:, hf * 512:(hf + 1) * 512], pb.t[:], npost[npi].t[:, hf * 512:(hf + 1) * 512], ALU.mult,
                                r=[pb, npost[npi]], w=[G[w]])
        mod.update(modT=modT, A1=A1, A2=A2, G1=G1, G2=G2)
        return mod

    def norm_transpose(self, xt, A, sh_c0, modT, w, hT, col0, tmp, pT):
        junk, ss, rstd, xn = tmp["junk"], tmp["ss"], tmp["rstd"], tmp["xn"]
        self.act(junk.t[:], xt.t[:], AF.Square, r=[xt], w=[junk, ss], accum_out=ss.t[:, 0:1])
        self.act(rstd.t[:, 0:1], ss.t[:, 0:1], AF.Sqrt, r=[ss], w=[rstd], bias=EPS, scale=1.0 / D)
        self.recip(rstd.t[:, 0:1], rstd.t[:, 0:1], r=[rstd], w=[rstd])
        self.ts("vector", xn.t[:], xt.t[:], rstd.t[:, 0:1], None, ALU.mult, r=[xt, rstd], w=[xn])
        pv = pT.t[:].bitcast(BF16).rearrange("p (c n) -> p c n", n=128)
        for c in range(8):
            self.tr(pv[:, c, :], xn.t[:, c * 128:(c + 1) * 128], self.ident.t[:], r=[xn, self.ident], w=[pT])
        self.ntc = getattr(self, "ntc", 0) + 1
        for c in range(8):
            o = hT.t[:, c, col0:col0 + 128]
            if self.ntc % 2 == 0:
                self.act(o, pv[:, c, :], AF.Identity, r=[pT, A, modT], w=[hT], scale=A.t[:, c, w:w + 1],
                         bias=modT.t[:, sh_c0 + c, w:w + 1])
            else:
                self.ts("vector", o, pv[:, c, :], A.t[:, c, w:w + 1], modT.t[:, sh_c0 + c, w:w + 1], ALU.mult, ALU.add,
                        r=[pT, A, modT], w=[hT])

    def stage_proj(self, l, mod, xsrc, csrc, sc, last):
        I = self.inputs
        S, T = self.S, self.T
        with ExitStack() as es:
            ps = self.psum_banks(es, 8)
            rot = [0]

            def nb():
                rot[0] = (rot[0] % 7) + 1
                return ps[rot[0]]

            win = self.sb(es, "win", [128, 8, 2848], BF16)
            for k in range(8):
                self.dma("gpsimd", win.t[:, k, :], I["w_in"].t[l][k * 128:(k + 1) * 128, 0:2848], r=[I["w_in"]], w=[win])
            wkr = self.sb(es, "wkr", [128, 8, 192], BF16)
            self.dma("gpsimd", wkr.t[:], I["w_kr96"].t[l].rearrange("(k p) n -> p k n", p=128), r=[I["w_kr96"]], w=[wkr])
            wuq = self.sb(es, "wuq", [128, 4, 1536], BF16)
            self.dma("gpsimd", wuq.t[:], I["w_uq2"].t[l].rearrange("(k p) n -> p k n", p=128), r=[I["w_uq2"]], w=[wuq])
            wukk = self.sb(es, "wukk", [128, 2, 512], BF16)
            self.dma("gpsimd", wukk.t[:], I["w_ukv_k"].t[l].rearrange("(k p) n -> p k n", p=128), r=[I["w_ukv_k"]], w=[wukk])
            wukv = self.sb(es, "wukv", [128, 2, 512], BF16)
            self.dma("gpsimd", wukv.t[:], I["w_ukv_v"].t[l].rearrange("(k p) n -> p k n", p=128), r=[I["w_ukv_v"]], w=[wukv])
            qnT = self.sb(es, "qnT", [128, 4], F32)
            self.dma("sync", qnT.t[:], I["qnT"].t[l], r=[I["qnT"]], w=[qnT])
            kvnT = self.sb(es, "kvnT", [128, 2], F32)
            self.dma("sync", kvnT.t[:], I["kvnT"].t[l], r=[I["kvnT"]], w=[kvnT])
            xts = [self.sb(es, "xt%d" % i, [128, D], F32) for i in range(3)]
            tmps = [dict(junk=self.sb(es, "junk", [128, D], BF16), ss=self.sb(es, "ss", [128, 1], F32),
                         rstd=self.sb(es, "rstd", [128, 1], F32), xn=self.sb(es, "xn", [128, D], BF16)) for i in range(2)]
            hTs = [self.sb(es, "hT%d" % i, [128, 8, 512], BF16) for i in range(2)]
            stg = [self.sb(es, "stg%d" % i, [128, 512], BF16) for i in range(6)]
            stgf = [self.sb(es, "stgf%d" % i, [128, 512], F32) for i in range(2)]
            cqT = self.sb(es, "cqT", [128, 4, 512], F32)
            sq = self.sb(es, "sq", [128, 4, 512], BF16)
            ckvT = self.sb(es, "ckvT", [128, 2, 512], F32)
            sqkv = self.sb(es, "sqkv", [128, 2, 512], BF16)
            rbc = [self.sb(es, "rbc%d" % i, [128, 512], F32) for i in range(2)]
            cqn = self.sb(es, "cqn", [128, 4, 512], BF16)
            ckvn = self.sb(es, "ckvn", [128, 2, 512], BF16)
            rq = [self.sb(es, "rq%d" % i, [96, 2, 512], F32) for i in range(2)]
            rk = [self.sb(es, "rk%d" % i, [96, 2, 512], F32) for i in range(2)]
            rt = [self.sb(es, "rt%d" % i, [96, 512], F32) for i in range(4)]
            cnt = dict(stg=0, stgf=0, x=0, rt=0, ev=0)

            def nstg():
                cnt["stg"] += 1
                return stg[cnt["stg"] % 6]

            def evac(out_ap, in_ap, r, w, scale=None):
                cnt["ev"] += 1
                if cnt["ev"] % 2 == 0:
                    if scale is None:
                        self.copy("scalar", out_ap, in_ap, r=r, w=w)
                    else:
                        self.act(out_ap, in_ap, AF.Identity, r=r, w=w, scale=scale)
                else:
                    if scale is None:
                        self.copy("vector", out_ap, in_ap, r=r, w=w)
                    else:
                        self.ts("vector", out_ap, in_ap, scale, None, ALU.mult, r=r, w=w)

            mi = 0
            for (w, s0, slen) in [(0, 0, S), (1, S, CTX)]:
                src = xsrc if w == 0 else csrc
                for (c0, n) in self.macro_tiles(s0, slen):
                    mi += 1
                    nt = n // 128
                    hT = hTs[mi % 2]
                    for i in range(nt):
                        cnt["x"] += 1
                        xt = xts[cnt["x"] % 3]
                        r0 = c0 - s0 + i * 128
                        self.dma("sync", xt.t[:], src.t[r0:r0 + 128, :], r=[src], w=[xt])
                        self.norm_transpose(xt, mod["A1"], 0, mod["modT"], w, hT, i * 128, tmps[cnt["x"] % 2], ps[0])
                    self.dma("gpsimd", sc["hT"].t[:, c0:c0 + n].rearrange("(k p) t -> p k t", p=128), hT.t[:, :, 0:n], r=[hT], w=[sc["hT"]])

                    def fm(wt, col0, M):
                        pb = nb()
                        for k in range(8):
                            self.mm(pb.t[0:M, 0:n], wt.t[:, k, col0:col0 + M], hT.t[:, k, 0:n], start=(k == 0), stop=(k == 7),
                                    r=[wt, hT], w=[pb])
                        return pb
                    if w == 0 or not last:
                        for cc in range(4):
                            pb = fm(win, cc * 128, 128)
                            st = nstg()
                            evac(st.t[:, 0:n], pb.t[:, 0:n], [pb], [st], scale=0.125)
                            self.dma("gpsimd", sc["naq"].t[2 * cc:2 * cc + 2, :, c0:c0 + n].rearrange("h d t -> (h d) t"), st.t[:, 0:n],
                                     r=[st], w=[sc["naq"]])
                    for cc in range(4):
                        pb = fm(win, 512 + cc * 128, 128)
                        st = nstg()
                        evac(st.t[:, 0:n], pb.t[:, 0:n], [pb], [st])
                        self.dma("gpsimd", sc["nak"].t[2 * cc:2 * cc + 2, :, c0:c0 + n].rearrange("h d t -> (h d) t"), st.t[:, 0:n],
                                 r=[st], w=[sc["nak"]])
                    for i in range(nt):
                        pb = nb()
                        for k in range(8):
                            self.mm(pb.t[:, :], hT.t[:, k, i * 128:(i + 1) * 128], win.t[:, k, 1024:1536], start=(k == 0), stop=(k == 7),
                                    r=[win, hT], w=[pb])
                        st = nstg()
                        evac(st.t[:], pb.t[:], [pb], [st])
                        self.dma("gpsimd", sc["nav"].t[c0 + i * 128:c0 + (i + 1) * 128, :], st.t[:], r=[st], w=[sc["nav"]])
                    if w == 0 or not last:
                        for cc in range(4):
                            pb = fm(win, 1536 + cc * 128, 128)
                            cnt["stgf"] += 1
                            st = stgf[cnt["stgf"] % 2]
                            evac(st.t[:, 0:n], pb.t[:, 0:n], [pb], [st])
                            self.dma("gpsimd", sc["poolu"].t[cc * 128:(cc + 1) * 128, c0:c0 + n], st.t[:, 0:n], r=[st], w=[sc["poolu"]])
                    if w == 0:
                        rqt, rkt = rq[mi % 2], rk[mi % 2]
                        self.dma("sync", rqt.t[64:96, :, 0:n], I["ropeq"].t[:, :, c0:c0 + n].rearrange("v d t -> d v t"), r=[I["ropeq"]], w=[rqt])
                        self.dma("sync", rkt.t[64:96, :, 0:n], I["ropek"].t[:, :, c0:c0 + n].rearrange("v d t -> d v t"), r=[I["ropek"]], w=[rkt])

                    def rope_combine(p0, p1, tab, out_ap, outbuf, scale_ctx):
                        if w == 0:
                            cnt["rt"] += 1
                            t1 = rt[cnt["rt"] % 4]
                            cnt["rt"] += 1
                            t2 = rt[cnt["rt"] % 4]
                            self.tt("vector", t1.t[64:96, 0:n], p0.t[64:96, 0:n], tab.t[64:96, 0, 0:n], ALU.mult, r=[p0, tab], w=[t1])
                            self.tt("vector", t2.t[64:96, 0:n], p1.t[64:96, 0:n], tab.t[64:96, 1, 0:n], ALU.mult, r=[p1, tab], w=[t2])
                            self.tt("gpsimd", out_ap, t1.t[64:96, 0:n], t2.t[64:96, 0:n], ALU.add, r=[t1, t2], w=[outbuf])
                        else:
                            if scale_ctx is None:
                                self.copy("vector", out_ap, p0.t[64:96, 0:n], r=[p0], w=[outbuf])
                            else:
                                self.act(out_ap, p0.t[64:96, 0:n], AF.Identity, r=[p0], w=[outbuf], scale=scale_ctx)
                    p0 = fm(wkr, 0, 96)
                    p1 = fm(wkr, 96, 96) if w == 0 else None
                    st = nstg()
                    rope_combine(p0, p1, rk[mi % 2], st.t[64:96, 0:n], st, None)
                    for h in range(NH):
                        self.dma("gpsimd", sc["km"].t[h, 64:96, c0:c0 + n], st.t[64:96, 0:n], r=[st], w=[sc["km"]])
                    if w == 0 or not last:
                        for cc in range(4):
                            pb = fm(win, 2048 + cc * 128, 128)
                            self.copy("vector", cqT.t[:, cc, 0:n], pb.t[:, 0:n], r=[pb], w=[cqT])
                            self.act(sq.t[:, cc, 0:n], cqT.t[:, cc, 0:n], AF.Square, r=[cqT], w=[sq])
                        pb = nb()
                        for cc in range(4):
                            self.mm(pb.t[:, 0:n], self.ones.t[:], sq.t[:, cc, 0:n], start=(cc == 0), stop=(cc == 3), r=[self.ones, sq], w=[pb])
                        self.act(rbc[0].t[:, 0:n], pb.t[:, 0:n], AF.Sqrt, r=[pb], w=[rbc[0]], bias=EPS, scale=1.0 / 512)
                        self.recip(rbc[0].t[:, 0:n], rbc[0].t[:, 0:n], r=[rbc[0]], w=[rbc[0]])
                        for cc in range(4):
                            self.stt("vector", cqn.t[:, cc, 0:n], cqT.t[:, cc, 0:n], qnT.t[:, cc:cc + 1], rbc[0].t[:, 0:n],
                                     ALU.mult, ALU.mult, r=[cqT, qnT, rbc[0]], w=[cqn])
                        qscale = float(96 ** -0.5)
                        for h in range(NH):
                            p0 = nb()
                            for cc in range(4):
                                self.mm(p0.t[0:96, 0:n], wuq.t[:, cc, h * 96:(h + 1) * 96], cqn.t[:, cc, 0:n], start=(cc == 0), stop=(cc == 3),
                                        r=[wuq, cqn], w=[p0])
                            p1 = None
                            if w == 0:
                                p1 = nb()
                                for cc in range(4):
                                    self.mm(p1.t[0:96, 0:n], wuq.t[:, cc, 768 + h * 96:768 + (h + 1) * 96], cqn.t[:, cc, 0:n],
                                            start=(cc == 0), stop=(cc == 3), r=[wuq, cqn], w=[p1])
                            st = nstg()
                            self.act(st.t[0:64, 0:n], p0.t[0:64, 0:n], AF.Identity, r=[p0], w=[st], scale=qscale)
                            rope_combine(p0, p1, rq[mi % 2], st.t[64:96, 0:n], st, qscale)
                            self.dma("gpsimd", sc["qm"].t[h, :, c0:c0 + n], st.t[0:96, 0:n], r=[st], w=[sc["qm"]])
                    for cc in range(2):
                        pb = fm(win, 2560 + cc * 128, 128)
                        self.copy("vector", ckvT.t[:, cc, 0:n], pb.t[:, 0:n], r=[pb], w=[ckvT])
                        self.act(sqkv.t[:, cc, 0:n], ckvT.t[:, cc, 0:n], AF.Square, r=[ckvT], w=[sqkv])
                    pb = nb()
                    for cc in range(2):
                        self.mm(pb.t[:, 0:n], self.ones.t[:], sqkv.t[:, cc, 0:n], start=(cc == 0), stop=(cc == 1), r=[self.ones, sqkv], w=[pb])
                    self.act(rbc[1].t[:, 0:n], pb.t[:, 0:n], AF.Sqrt, r=[pb], w=[rbc[1]], bias=EPS, scale=1.0 / 256)
                    self.recip(rbc[1].t[:, 0:n], rbc[1].t[:, 0:n], r=[rbc[1]], w=[rbc[1]])
                    for cc in range(2):
                        self.stt("vector", ckvn.t[:, cc, 0:n], ckvT.t[:, cc, 0:n], kvnT.t[:, cc:cc + 1], rbc[1].t[:, 0:n],
                                 ALU.mult, ALU.mult, r=[ckvT, kvnT, rbc[1]], w=[ckvn])
                    for hp in range(4):
                        pb = nb()
                        for cc in range(2):
                            self.mm(pb.t[:, 0:n], wukk.t[:, cc, hp * 128:(hp + 1) * 128], ckvn.t[:, cc, 0:n], start=(cc == 0), stop=(cc == 1),
                                    r=[wukk, ckvn], w=[pb])
                        st = nstg()
                        evac(st.t[:, 0:n], pb.t[:, 0:n], [pb], [st])
                        for j in range(2):
                            self.dma("gpsimd", sc["km"].t[2 * hp + j, 0:64, c0:c0 + n], st.t[j * 64:(j + 1) * 64, 0:n], r=[st], w=[sc["km"]])
                    for i in range(nt):
                        pb = nb()
                        for cc in range(2):
                            self.mm(pb.t[:, :], ckvn.t[:, cc, i * 128:(i + 1) * 128], wukv.t[:, cc, :], start=(cc == 0), stop=(cc == 1),
                                    r=[wukv, ckvn], w=[pb])
                        st = nstg()
                        evac(st.t[:], pb.t[:], [pb], [st])
                        tt_i = (c0 + i * 128) // 128
                        self.dma("gpsimd", sc["vm"].t[:, :, tt_i, :].rearrange("h p d -> p h d"),
                                 st.t[:].rearrange("p (h d) -> p h d", d=64), r=[st], w=[sc["vm"]])

    def stage_pool(self, l, sc, last):
        I = self.inputs
        S = self.S
        with ExitStack() as es:
            ps = self.psum_banks(es, 4)
            LM = S
            U = self.sb(es, "pU", [128, LM + 16], F32)
            A = self.sb(es, "pA", [128, LM + 16], F32)
            Bt = self.sb(es, "pB", [128, LM + 16], F32)
            inv = self.sb(es, "pinv", [128, LM], F32)
            dd = self.sb(es, "pd", [128, LM], BF16)
            pw = self.sb(es, "pw", [128, 4, 128], BF16)
            self.dma("gpsimd", pw.t[:], I["pool_w"].t[l].rearrange("g i o -> i g o"), r=[I["pool_w"]], w=[pw])
            pscT = self.sb(es, "pscT", [128, 4], F32)
            self.dma("sync", pscT.t[:], I["pool_scaleT"].t[l], r=[I["pool_scaleT"]], w=[pscT])
            stg = [self.sb(es, "pstg%d" % i, [128, 512], BF16) for i in range(3)]
            k = 0
            for (w, s0, Ln) in self.segs(last):
                for g in range(4):
                    self.memset("vector", U.t[:, 0:8], 0.0, w=[U])
                    self.memset("vector", U.t[:, 8 + Ln:16 + Ln], 0.0, w=[U])
                    self.dma("sync", U.t[:, 8:8 + Ln], sc["poolu"].t[g * 128:(g + 1) * 128, s0:s0 + Ln], r=[sc["poolu"]], w=[U])
                    self.dma("sync", inv.t[:, 0:Ln], I["pool_inv"].t[g, :, s0:s0 + Ln].partition_broadcast(128), r=[I["pool_inv"]], w=[inv])
                    self.tt("vector", A.t[:, 1:Ln + 15], U.t[:, 0:Ln + 14], U.t[:, 1:Ln + 15], ALU.add, r=[U], w=[A])
                    cur = A
                    oth = Bt
                    lo, hi, sh = 1, Ln + 15, 1
                    for step in range(g):
                        lo2, hi2 = lo + sh, hi - sh
                        self.tt("gpsimd" if step % 2 == 0 else "vector", oth.t[:, lo2:hi2], cur.t[:, lo2 - sh:hi2 - sh], cur.t[:, lo2 + sh:hi2 + sh], ALU.add,
                                r=[cur], w=[oth])
                        cur, oth = oth, cur
                        lo, hi, sh = lo2, hi2, sh * 2
                    self.tt("vector", oth.t[:, 8:8 + Ln], cur.t[:, 8:8 + Ln], inv.t[:, 0:Ln], ALU.mult, r=[cur, inv], w=[oth])
                    self.tt("gpsimd", dd.t[:, 0:Ln], oth.t[:, 8:8 + Ln], U.t[:, 8:8 + Ln], ALU.subtract, r=[oth, U], w=[dd])
                    for (c0, n) in self.macro_tiles(0, Ln):
                        k += 1
                        pb = ps[k % 4]
                        self.mm(pb.t[:, 0:n], pw.t[:, g, :], dd.t[:, c0:c0 + n], r=[pw, dd], w=[pb])
                        st = stg[k % 3]
                        self.act(st.t[:, 0:n], pb.t[:, 0:n], AF.Identity, r=[pb, pscT], w=[st], scale=pscT.t[:, g:g + 1])
                        self.dma("gpsimd", sc["opool"].t[g * 128:(g + 1) * 128, s0 + c0:s0 + c0 + n], st.t[:, 0:n], r=[st], w=[sc["opool"]])

    def attn(self, st, q_ap, qbufs, keys, n, out_dram_ap, out_buf):
        ps = st["ps"]
        st["o"] += 1
        po = ps[st["o"] % 2]
        nk = len(keys)
        groups = [list(range(a, min(a + 2, nk))) for a in range(0, nk, 2)]
        G = len(groups)
        Pl = [None] * G

        def issue_s(g):
            st["s"] = (st["s"] + 1) % 2
            pS = st["pair"][st["s"]]
            pv3 = pS.t[:].rearrange("p (a b) -> p a b", b=512)
            for a, j in enumerate(groups[g]):
                kT, kbufs, va, vbufs, bias, bbufs = keys[j]
                self.mm(pv3[:, a, 0:n], kT, q_ap, start=True, stop=(bias is None), r=list(kbufs) + list(qbufs), w=[pS])
                if bias is not None:
                    self.mm(pv3[:, a, 0:n], self.ident.t[:], bias, start=False, stop=True, r=[self.ident] + list(bbufs), w=[pS])
            st["p"] = (st["p"] + 1) % len(st["P"])
            P = st["P"][st["p"]]
            na = len(groups[g])
            if n == 512:
                self.act(P.t[:, 0:na, :].rearrange("p a b -> p (a b)"), pS.t[:, 0:na * 512], AF.Exp, r=[pS], w=[P])
            else:
                self.act(P.t[:, 0:na, 0:n], pv3[:, 0:na, 0:n], AF.Exp, r=[pS], w=[P])
            Pl[g] = P

        def issue_pv(g):
            for a, j in enumerate(groups[g]):
                kT, kbufs, va, vbufs, bias, bbufs = keys[j]
                self.mm(po.t[0:65, 0:n], va, Pl[g].t[:, a, 0:n], start=(j == 0), stop=(j == nk - 1), r=list(vbufs) + [Pl[g]], w=[po])

        for g in range(min(2, G)):
            issue_s(g)
        if st.get("pending") is not None:
            st["pending"]()
            st["pending"] = None
        for g in range(G):
            issue_pv(g)
            if g + 2 < G:
                issue_s(g + 2)

        def finalize():
            st["f"] += 1
            osb = st["osb"][st["f"] % 2]
            self.copy("vector", osb.t[0:65, 0:n], po.t[0:65, 0:n], r=[po], w=[osb])
            pb = st["fb"]
            self.mm(pb.t[0:64, 0:n], self.onesf.t[64:65, 0:64], osb.t[64:65, 0:n], r=[self.onesf, osb], w=[pb])
            rc = st["rc"][st["f"] % 2]
            self.recip(rc.t[0:64, 0:n], pb.t[0:64, 0:n], r=[pb], w=[rc])
            og = st["og"][st["f"] % 2]
            self.tt("gpsimd", og.t[0:64, 0:n], osb.t[0:64, 0:n], rc.t[0:64, 0:n], ALU.mult, r=[osb, rc], w=[og])
            self.dma("gpsimd", out_dram_ap, og.t[0:64, 0:n], r=[og], w=[out_buf])
        st["pending"] = finalize

    def attn_flush(self, st):
        if st.get("pending") is not None:
            st["pending"]()
            st["pending"] = None

    def attn_state(self, es):
        ps = self.psum_banks(es, 3)
        pair = []
        for i in range(2):
            self.uid += 1
            t = es.enter_context(self.nc.psum_tensor("pair%d_%d" % (i, self.uid), [128, 1024], F32))
            pair.append(TT(t, "pair%d" % i))
            pair[-1].b.excl = True
        return dict(ps=ps, pair=pair, o=0, s=0, p=0, f=0, pending=None, fb=ps[2],
                    P=[self.sb(es, "P%d" % i, [128, 2, 512], BF16) for i in range(4)],
                    osb=[self.sb(es, "osb%d" % i, [128, 512], F32) for i in range(2)],
                    rc=[self.sb(es, "rc%d" % i, [64, 512], F32) for i in range(2)],
                    og=[self.sb(es, "og%d" % i, [64, 512], BF16) for i in range(2)])

    def stage_na(self, l, sc, last):
        I = self.inputs
        S, T, NT, NM = self.S, self.T, self.NT, self.NM
        with ExitStack() as es:
            st = self.attn_state(es)
            kc = self.sb(es, "nakc", [64, NH, CTX], BF16)
            self.dma("sync", kc.t[:], sc["nak"].t[:, :, S:S + CTX].rearrange("h d t -> d h t"), r=[sc["nak"]], w=[kc])
            vc = self.sb(es, "navc", [128, 2, NH, 65], BF16)
            self.memset("vector", vc.t[:, :, :, 64:65], 1.0, w=[vc])
            for j in range(2):
                self.dma("sync", vc.t[:, j, :, 0:64], sc["nav"].t[S + j * 128:S + (j + 1) * 128, :].rearrange("p (h d) -> p h d", d=64), r=[sc["nav"]], w=[vc])
            bt = self.sb(es, "nabt", [128, 8, NH, 512], BF16)
            kws = [self.sb(es, "nakw%d" % i, [64, NH, 8 * 128], BF16) for i in range(2)]
            vws = [self.sb(es, "navw%d" % i, [128, 8, NH, 65], BF16) for i in range(2)]
            for v in vws:
                self.memset("vector", v.t[:, :, :, 64:65], 1.0, w=[v])
            qs = [self.sb(es, "naq%d" % i, [64, NH, 512], BF16) for i in range(2)]
            loaded = None
            for m in range(NM):
                kt0, kt1 = max(0, 4 * m - 2), min(NT - 1, 4 * m + 5)
                nk = kt1 - kt0 + 1
                if m == 0:
                    kind, base, d0 = "first", 8, 0
                elif m == NM - 1:
                    kind, base, d0 = "last", 14, -2
                else:
                    kind, base, d0 = "int", 0, -2
                if loaded != kind:
                    nsl = 8 if kind == "int" else 6
                    R = self.R
                    for sl in range(nsl):
                        for kr in range(2):
                            for qr in range(8):
                                if kind == "int":
                                    dr = 2 * (sl + d0) + kr - qr
                                    blk = dr + 7 if -4 <= dr <= 3 else 15
                                else:
                                    key_row = 2 * (4 * m + sl + d0) + kr
                                    r = 8 * m + qr
                                    r0 = min(max(r - 4, 0), R - 8)
                                    blk = (key_row - r + 7) if (r0 <= key_row < r0 + 8) else 15
                                self.dma("gpsimd", bt.t[kr * 64:(kr + 1) * 64, sl, :, qr * 64:(qr + 1) * 64],
                                         I["na_blk"].t[l, :, blk].rearrange("h k q -> k h q"), r=[I["na_blk"]], w=[bt])
                    loaded = kind
                kw, vw, q = kws[m % 2], vws[m % 2], qs[m % 2]
                self.dma("sync", kw.t[:, :, 0:nk * 128], sc["nak"].t[:, :, kt0 * 128:(kt1 + 1) * 128].rearrange("h d t -> d h t"), r=[sc["nak"]], w=[kw])
                for j in range(nk):
                    self.dma("sync", vw.t[:, j, :, 0:64], sc["nav"].t[(kt0 + j) * 128:(kt0 + j + 1) * 128, :].rearrange("p (h d) -> p h d", d=64),
                             r=[sc["nav"]], w=[vw])
                self.dma("sync", q.t[:], sc["naq"].t[:, :, m * 512:(m + 1) * 512].rearrange("h d t -> d h t"), r=[sc["naq"]], w=[q])
                for h in range(NH):
                    keys = []
                    for kt in range(kt0, kt1 + 1):
                        sl = (kt - 4 * m) - d0
                        keys.append((kw.t[:, h, (kt - kt0) * 128:(kt - kt0 + 1) * 128], [kw], vw.t[:, kt - kt0, h, :], [vw], bt.t[:, sl, h, :], [bt]))
                    for j in range(2):
                        keys.append((kc.t[:, h, j * 128:(j + 1) * 128], [kc], vc.t[:, j, h, :], [vc], None, []))
                    self.attn(st, q.t[:, h, :], [q], keys, 512, sc["ona"].t[h * 64:(h + 1) * 64, m * 512:(m + 1) * 512], sc["ona"])
            if not last:
                q = qs[NM % 2]
                self.dma("sync", q.t[:, :, 0:CTX], sc["naq"].t[:, :, S:S + CTX].rearrange("h d t -> d h t"), r=[sc["naq"]], w=[q])
                for h in range(NH):
                    keys = [(kc.t[:, h, j * 128:(j + 1) * 128], [kc], vc.t[:, j, h, :], [vc], None, []) for j in range(2)]
                    self.attn(st, q.t[:, h, 0:CTX], [q], keys, CTX, sc["ona"].t[h * 64:(h + 1) * 64, S:S + CTX], sc["ona"])
            self.attn_flush(st)

    def stage_mla(self, l, sc, last):
        S, T = self.S, self.T
        NTT = T // 128
        with ExitStack() as es:
            st = self.attn_state(es)
            Ks = [self.sb(es, "mK%d" % i, [96, T], BF16) for i in range(2)]
            Qs = [self.sb(es, "mQ%d" % i, [96, T], BF16) for i in range(2)]
            Vs = [self.sb(es, "mV%d" % i, [128, NTT, 65], BF16) for i in range(2)]
            for v in Vs:
                self.memset("vector", v.t[:, :, 64:65], 1.0, w=[v])
            for h in range(NH):
                K, Q, V = Ks[h % 2], Qs[h % 2], Vs[h % 2]
                self.dma("sync", K.t[:], sc["km"].t[h], r=[sc["km"]], w=[K])
                nq = S if last else T
                self.dma("sync", Q.t[:, 0:nq], sc["qm"].t[h, :, 0:nq], r=[sc["qm"]], w=[Q])
                self.dma("sync", V.t[:, :, 0:64], sc["vm"].t[h], r=[sc["vm"]], w=[V])
                allk = [(K.t[:, j * 128:(j + 1) * 128], [K], V.t[:, j, :], [V], None, []) for j in range(NTT)]
                for (c0, n) in self.macro_tiles(0, S):
                    self.attn(st, Q.t[:, c0:c0 + n], [Q], allk, n, sc["omla"].t[h * 64:(h + 1) * 64, c0:c0 + n], sc["omla"])
                if not last:
                    self.attn(st, Q.t[:, S:T], [Q], allk[S // 128:], CTX, sc["omla"].t[h * 64:(h + 1) * 64, S:T], sc["omla"])
            self.attn_flush(st)

    def post_norm_residual(self, ybanks, xres, G, xo, tmp):
        ss2, rstd, junk, t1 = tmp["ss2"], tmp["rstd"], tmp["junkf"], tmp["t1"]
        for hf in range(2):
            self.act(junk.t[:, 0:512], ybanks[hf].t[:], AF.Square, r=[ybanks[hf]], w=[junk, ss2], accum_out=ss2.t[:, hf:hf + 1])
        self.tt("vector", ss2.t[:, 2:3], ss2.t[:, 0:1], ss2.t[:, 1:2], ALU.add, r=[ss2], w=[ss2])
        self.act(rstd.t[:, 0:1], ss2.t[:, 2:3], AF.Sqrt, r=[ss2], w=[rstd], bias=EPS, scale=1.0 / D)
        self.recip(rstd.t[:, 0:1], rstd.t[:, 0:1], r=[rstd], w=[rstd])
        for hf in range(2):
            sl = slice(hf * 512, (hf + 1) * 512)
            self.stt("vector", t1.t[:, sl], ybanks[hf].t[:], rstd.t[:, 0:1], G.t[:, sl], ALU.mult, ALU.mult, r=[ybanks[hf], rstd, G], w=[t1])
        self.tt("gpsimd", xo.t[:], t1.t[:], xres.t[:], ALU.add, r=[t1, xres], w=[xo])

    def stage_merge(self, l, mod, xsrc, csrc, sc, last):
        I = self.inputs
        S, T = self.S, self.T
        with ExitStack() as es:
            ps = self.psum_banks(es, 8)
            pT = ps[7]
            ybanks = ps[5:7]
            wg = self.sb(es, "wg", [128, 8, 3072], BF16)
            for k in range(8):
                self.dma("gpsimd", wg.t[:, k, :], I["w_in"].t[l][k * 128:(k + 1) * 128, 2848:IN_COLS], r=[I["w_in"]], w=[wg])
            wbr = self.sb(es, "wbr", [128, 12, D], BF16)
            self.dma("gpsimd", wbr.t[:], I["w_branch"].t[l].rearrange("b (c p) n -> p (b c) n", p=128), r=[I["w_branch"]], w=[wbr])
            wo = self.sb(es, "wo", [128, 8, D], BF16)
            self.dma("gpsimd", wo.t[:], I["w_o"].t[l].rearrange("(c p) n -> p c n", p=128), r=[I["w_o"]], w=[wo])
            hTs = [self.sb(es, "mhT%d" % i, [128, 8, 512], BF16) for i in range(2)]
            brs = [[self.sb(es, "mbr%d_%d" % (b, i), [128, 4, 512], BF16) for b in range(3)] for i in range(2)]
            sig = [self.sb(es, "sig%d" % i, [128, 512], F32) for i in range(3)]
            ta = [self.sb(es, "mta%d" % i, [128, 512], F32) for i in range(3)]
            mg = self.sb(es, "mg", [128, 8, 512], BF16)
            xts = [self.sb(es, "mxt%d" % i, [128, D], F32) for i in range(2)]
            xos = [self.sb(es, "mxo%d" % i, [128, D], F32) for i in range(2)]
            tmp = dict(ss2=self.sb(es, "mss2", [128, 3], F32), rstd=self.sb(es, "mrstd", [128, 1], F32),
                       junkf=self.sb(es, "mjunkf", [128, 512], BF16), t1=self.sb(es, "mt1", [128, D], F32))
            tmp2 = dict(junk=self.sb(es, "mjunk", [128, D], BF16), ss=self.sb(es, "mss", [128, 1], F32),
                        rstd=self.sb(es, "mrstd2", [128, 1], F32), xn=self.sb(es, "mxn", [128, D], BF16))
            h2s = [self.sb(es, "mh2%d" % i, [128, 8, 128], BF16) for i in range(2)]
            srcs = [("ona", 0), ("opool", 1), ("omla", 2)]
            mi = 0
            ti = 0
            for (w, s0, slen) in self.segs(last):
                src = xsrc if w == 0 else csrc
                for (c0, n) in self.macro_tiles(s0, slen):
                    mi += 1
                    hT = hTs[mi % 2]
                    br = brs[mi % 2]
                    self.dma("sync", hT.t[:, :, 0:n], sc["hT"].t[:, c0:c0 + n].rearrange("(k p) t -> p k t", p=128), r=[sc["hT"]], w=[hT])
                    for (nm, b) in srcs:
                        self.dma("sync", br[b].t[:, :, 0:n], sc[nm].t[:, c0:c0 + n].rearrange("(k p) t -> p k t", p=128), r=[sc[nm]], w=[br[b]])
                    for c in range(8):
                        for b in range(3):
                            pg = ps[b % 2]
                            for k in range(8):
                                self.mm(pg.t[:, 0:n], wg.t[:, k, b * D + c * 128:b * D + (c + 1) * 128], hT.t[:, k, 0:n], start=(k == 0), stop=(k == 7),
                                        r=[wg, hT], w=[pg])
                            self.act(sig[b].t[:, 0:n], pg.t[:, 0:n], AF.Sigmoid, r=[pg], w=[sig[b]])
                            pp = ps[2 + b]
                            for k in range(4):
                                self.mm(pp.t[:, 0:n], wbr.t[:, b * 4 + k, c * 128:(c + 1) * 128], br[b].t[:, k, 0:n], start=(k == 0), stop=(k == 3),
                                        r=[wbr, br[b]], w=[pp])
                            self.tt("vector", ta[b].t[:, 0:n], pp.t[:, 0:n], sig[b].t[:, 0:n], ALU.mult, r=[pp, sig[b]], w=[ta[b]])
                        self.tt("gpsimd", ta[0].t[:, 0:n], ta[0].t[:, 0:n], ta[1].t[:, 0:n], ALU.add, r=[ta[0], ta[1]], w=[ta[0]])
                        self.tt("gpsimd", mg.t[:, c, 0:n], ta[0].t[:, 0:n], ta[2].t[:, 0:n], ALU.add, r=[ta[0], ta[2]], w=[mg])
                    for i in range(n // 128):
                        ti += 1
                        xt, xo = xts[ti % 2], xos[ti % 2]
                        r0 = c0 - s0 + i * 128
                        self.dma("sync", xt.t[:], src.t[r0:r0 + 128, :], r=[src], w=[xt])
                        for hf in range(2):
                            for c in range(8):
                                self.mm(ybanks[hf].t[:], mg.t[:, c, i * 128:(i + 1) * 128], wo.t[:, c, hf * 512:(hf + 1) * 512], start=(c == 0), stop=(c == 7),
                                        r=[mg, wo], w=[ybanks[hf]])
                        self.post_norm_residual(ybanks, xt, mod["G1"][w], xo, tmp)
                        self.dma("gpsimd", sc["xmid"].t[c0 + i * 128:c0 + (i + 1) * 128, :], xo.t[:], r=[xo], w=[sc["xmid"]])
                        h2 = h2s[ti % 2]
                        self.norm_transpose(xo, mod["A2"], 24, mod["modT"], w, h2, 0, tmp2, pT)
                        self.dma("gpsimd", sc["h2T"].t[:, c0 + i * 128:c0 + (i + 1) * 128].rearrange("(k p) t -> p k t", p=128), h2.t[:], r=[h2], w=[sc["h2T"]])

    def stage_ffn(self, l, mod, sc, xo_d, xco_d, last):
        I = self.inputs
        S, T = self.S, self.T
        with ExitStack() as es:
            ps = self.psum_banks(es, 8)
            ybanks = ps[0:2]
            rot = [0]

            def nb():
                rot[0] = (rot[0] + 1) % 6
                return ps[2 + rot[0]]

            wd = self.sb(es, "wd", [128, 22, D], BF16)
            self.dma("gpsimd", wd.t[:], I["w_down"].t[l].rearrange("(c p) n -> p c n", p=128), r=[I["w_down"]], w=[wd])
            cw = self.sb(es, "cw", [128, 44, 3], F32)
            self.dma("sync", cw.t[:], I["conv_wT"].t[l], r=[I["conv_wT"]], w=[cw])
            cb = self.sb(es, "cb", [128, 44], F32)
            self.dma("sync", cb.t[:], I["conv_bT"].t[l], r=[I["conv_bT"]], w=[cb])
            NB = 1024
            h2s = [self.sb(es, "fh2_%d" % i, [128, 8, NB + 2], BF16) for i in range(2)]
            wus = [self.sb(es, "fwu%d" % i, [128, 8, 256], BF16) for i in range(4)]
            nblocks = sum((slen + NB - 1) // NB for (_, _, slen) in self.segs(last))
            wtotal = 22 * nblocks
            wload = [0]
            us = [[self.sb(es, "fu%d_%d" % (i, j), [128, NB + 2], F32) for j in range(2)] for i in range(2)]
            accs = [[self.sb(es, "facc%d_%d" % (i, j), [128, NB], F32) for j in range(2)] for i in range(2)]
            actT = self.sb(es, "factT", [128, 22, NB], BF16)
            xts = [self.sb(es, "fxt%d" % i, [128, D], F32) for i in range(2)]
            xos = [self.sb(es, "fxo%d" % i, [128, D], F32) for i in range(2)]
            tmp = dict(ss2=self.sb(es, "fss2", [128, 3], F32), rstd=self.sb(es, "frstd", [128, 1], F32),
                       junkf=self.sb(es, "fjunkf", [128, 512], BF16), t1=self.sb(es, "ft1", [128, D], F32))
            bi = 0
            wi = 0
            ti = 0
            ev = 0
            for (w, s0, slen) in self.segs(last):
                o = 0
                while o < slen:
                    nbk = min(NB, slen - o)
                    b0 = s0 + o
                    bi += 1
                    h2 = h2s[bi % 2]
                    lo = max(b0 - 1, s0)
                    hi = min(b0 + nbk + 1, s0 + slen)
                    if lo == b0:
                        self.memset("vector", h2.t[:, :, 0:1], 0.0, w=[h2])
                    if hi == b0 + nbk:
                        self.memset("vector", h2.t[:, :, nbk + 1:nbk + 2], 0.0, w=[h2])
                    self.dma("sync", h2.t[:, :, lo - (b0 - 1):hi - (b0 - 1)], sc["h2T"].t[:, lo:hi].rearrange("(k p) t -> p k t", p=128),
                             r=[sc["h2T"]], w=[h2])
                    ncols = nbk + 2
                    pieces = []
                    pp = 0
                    npc = (ncols + 511) // 512
                    psz = (ncols + npc - 1) // npc
                    while pp < ncols:
                        pieces.append((pp, min(psz, ncols - pp)))
                        pp += psz
                    for i in range(22):
                        wi += 1
                        wu = wus[wi % 4]
                        while wload[0] < min(wi + 3, wtotal):
                            wload[0] += 1
                            ii = (wload[0] - 1) % 22
                            wn = wus[wload[0] % 4]
                            self.dma("gpsimd", wn.t[:, :, 0:128], I["w_up"].t[l][:, ii * 128:(ii + 1) * 128].rearrange("(k p) n -> p k n", p=128),
                                     r=[I["w_up"]], w=[wn])
                            self.dma("gpsimd", wn.t[:, :, 128:256], I["w_up"].t[l][:, DFF + ii * 128:DFF + (ii + 1) * 128].rearrange("(k p) n -> p k n", p=128),
                                     r=[I["w_up"]], w=[wn])
                        for ab in range(2):
                            u = us[ab][i % 2]
                            for (p0, pn) in pieces:
                                pb = nb()
                                for k in range(8):
                                    self.mm(pb.t[:, 0:pn], wu.t[:, k, ab * 128:(ab + 1) * 128], h2.t[:, k, p0:p0 + pn], start=(k == 0), stop=(k == 7),
                                            r=[wu, h2], w=[pb])
                                ev += 1
                                self.copy("scalar" if ev % 2 else "vector", u.t[:, p0:p0 + pn], pb.t[:, 0:pn], r=[pb], w=[u])
                            ch = ab * 22 + i
                            acc = accs[ab][i % 2]
                            self.ts("gpsimd", acc.t[:, 0:nbk], u.t[:, 0:nbk], cw.t[:, ch, 0:1], cb.t[:, ch:ch + 1], ALU.mult, ALU.add,
                                    r=[u, cw, cb], w=[acc])
                            self.stt("vector", acc.t[:, 0:nbk], u.t[:, 1:nbk + 1], cw.t[:, ch, 1:2], acc.t[:, 0:nbk], ALU.mult, ALU.add,
                                     r=[u, cw, acc], w=[acc])
                            self.stt("vector", acc.t[:, 0:nbk], u.t[:, 2:nbk + 2], cw.t[:, ch, 2:3], acc.t[:, 0:nbk], ALU.mult, ALU.add,
                                     r=[u, cw, acc], w=[acc])
                        aa, ab_ = accs[0][i % 2], accs[1][i % 2]
                        self.act(aa.t[:, 0:nbk], aa.t[:, 0:nbk], AF.Gelu_apprx_tanh, r=[aa], w=[aa])
                        self.tt("gpsimd", actT.t[:, i, 0:nbk], aa.t[:, 0:nbk], ab_.t[:, 0:nbk], ALU.mult, r=[aa, ab_], w=[actT])
                    for tt_i in range(nbk // 128):
                        ti += 1
                        xt, xo = xts[ti % 2], xos[ti % 2]
                        row = b0 + tt_i * 128
                        self.dma("sync", xt.t[:], sc["xmid"].t[row:row + 128, :], r=[sc["xmid"]], w=[xt])
                        for hf in range(2):
                            for i in range(22):
                                self.mm(ybanks[hf].t[:], actT.t[:, i, tt_i * 128:(tt_i + 1) * 128], wd.t[:, i, hf * 512:(hf + 1) * 512],
                                        start=(i == 0), stop=(i == 21), r=[actT, wd], w=[ybanks[hf]])
                        self.post_norm_residual(ybanks, xt, mod["G2"][w], xo, tmp)
                        if w == 0:
                            self.dma("gpsimd", xo_d.t[row:row + 128, :], xo.t[:], r=[xo], w=[xo_d])
                        else:
                            self.dma("gpsimd", xco_d.t[row - S:row - S + 128, :], xo.t[:], r=[xo], w=[xco_d])
                    o += nbk


def rope_tables(S):
    n_freq = 8
    inv = (10000.0 ** (-np.arange(n_freq, dtype=np.float32) / n_freq)).astype(np.float32)
    t = np.arange(S)
    row = (t // 64).astype(np.float32)
    col = (t % 64).astype(np.float32)
    ar = row[:, None] * inv[None, :]
    ac = col[:, None] * inv[None, :]
    cos = np.concatenate([np.cos(ar), np.cos(ar), np.cos(ac), np.cos(ac)], axis=1).T
    sin = np.concatenate([-np.sin(ar), np.sin(ar), -np.sin(ac), np.sin(ac)], axis=1).T
    k = np.stack([cos, sin]).astype(np.float32)
    q = (k * np.float32(96 ** -0.5)).astype(np.float32)
    return np.ascontiguousarray(q), np.ascontiguousarray(k)


ROPE_PERM = np.array(list(range(8, 16)) + list(range(0, 8)) + list(range(24, 32)) + list(range(16, 24)))


def pool_inv_table(S):
    out = np.zeros((4, 1, S + CTX), np.float32)
    for gi, win in enumerate((2, 4, 8, 16)):
        for (s0, L) in ((0, S), (S, CTX)):
            t = np.arange(L)
            lo = np.clip(t - win // 2, 0, L)
            hi = np.clip(t + win // 2, 0, L)
            out[gi, 0, s0:s0 + L] = 1.0 / (hi - lo).astype(np.float32)
    return out


def na_bias_blocks(rpb):
    kcol = np.arange(64)
    qcol = np.arange(64)
    win0 = np.clip(qcol - 8, 0, 48)
    in_col = (kcol[:, None] >= win0[None, :]) & (kcol[:, None] < win0[None, :] + 16)
    rel_c = np.clip(kcol[:, None] - qcol[None, :] + 15, 0, 30)
    out = np.full((8, 16, 64, 64), NEG, np.float32)
    for dr in range(15):
        out[:, dr] = np.where(in_col[None], rpb[:, dr][:, rel_c], np.float32(NEG))
    return out


def host_inputs(inp, b, S, L):
    f = np.float32
    g = {}
    g["x"] = np.ascontiguousarray(inp["x"][b, :S])
    g["ctx"] = np.ascontiguousarray(inp["ctx"][b])
    cc = np.stack([inp["c"][b], inp["c_ctx"]], axis=0)
    g["ccT"] = np.ascontiguousarray(cc.reshape(2, 8, 128).transpose(2, 1, 0))
    g["w_ada"] = np.ascontiguousarray(inp["w_ada"][:L])
    g["b_adaT"] = np.ascontiguousarray(inp["b_ada"][:L].reshape(L, 48, 128).transpose(0, 2, 1))
    g["b_ada"] = np.ascontiguousarray(inp["b_ada"][:L].reshape(L, 1, 6 * D))
    norms = np.stack([inp["norm_pre1"][:L], inp["norm_post1"][:L], inp["norm_pre2"][:L], inp["norm_post2"][:L]], axis=1)
    g["normsT"] = np.ascontiguousarray(norms.reshape(L, 4, 8, 128).transpose(0, 1, 3, 2))
    g["norms"] = np.ascontiguousarray(norms.reshape(L, 4, 1, D))
    w_in = inp["w_in"][:L]
    g["w_in"] = np.ascontiguousarray(w_in)
    kr = w_in[:, :, 2816:2848]
    dummy = w_in[:, :, 0:64]
    g["w_kr96"] = np.ascontiguousarray(np.concatenate([dummy, kr, dummy, kr[:, :, ROPE_PERM]], axis=2))
    wuq = inp["w_uq"][:L].reshape(L, 512, 8, 96)
    wuq_p = np.concatenate([wuq[..., :64], wuq[..., 64:][..., ROPE_PERM]], axis=-1)
    g["w_uq2"] = np.ascontiguousarray(np.concatenate([wuq.reshape(L, 512, 768), wuq_p.reshape(L, 512, 768)], axis=2))
    wukv = inp["w_ukv"][:L].reshape(L, 256, 8, 128)
    g["w_ukv_k"] = np.ascontiguousarray(wukv[..., :64].reshape(L, 256, 512))
    g["w_ukv_v"] = np.ascontiguousarray(wukv[..., 64:].reshape(L, 256, 512))
    g["qnT"] = np.ascontiguousarray(inp["mla_q_norm"][:L].reshape(L, 4, 128).transpose(0, 2, 1))
    g["kvnT"] = np.ascontiguousarray(inp["mla_kv_norm"][:L].reshape(L, 2, 128).transpose(0, 2, 1))
    g["pool_w"] = np.ascontiguousarray(inp["pool_w"][:L])
    g["pool_scaleT"] = np.ascontiguousarray(inp["pool_scale"][:L].reshape(L, 4, 128).transpose(0, 2, 1))
    g["w_branch"] = np.ascontiguousarray(inp["w_branch"][:L])
    g["w_o"] = np.ascontiguousarray(inp["w_o"][:L])
    g["w_up"] = np.ascontiguousarray(inp["w_up"][:L])
    g["conv_wT"] = np.ascontiguousarray(inp["conv_w"][:L].reshape(L, 3, 44, 128).transpose(0, 3, 2, 1))
    g["conv_bT"] = np.ascontiguousarray(inp["conv_b"][:L].reshape(L, 44, 128).transpose(0, 2, 1))
    g["w_down"] = np.ascontiguousarray(inp["w_down"][:L])
    g["na_blk"] = np.stack([na_bias_blocks(np.asarray(inp["na_rpb"][l], f)) for l in range(L)])
    q, k = rope_tables(S)
    g["ropeq"], g["ropek"] = q, k
    g["pool_inv"] = pool_inv_table(S)
    g["ident"] = np.eye(128, dtype=f)
    sel = np.zeros((2, 256), f)
    sel[0, :128] = 1.0
    sel[1, 128:] = 1.0
    g["sel"] = sel
    return {k2: np.asarray(v, f) for k2, v in g.items()}


_CACHE = {}


def run(inp, S, L, n_batch, debug=(), cores_per_batch=2, stop_after=None, trace=False):
    kb = KB(S, L, debug)
    kb.stop_after = stop_after
    nc = kb.build()
    in_maps = []
    shared = None
    for b in range(n_batch):
        hm = host_inputs(inp, b, S, L)
        if shared is None:
            shared = hm
        else:
            for k in hm:
                if k not in ("x", "ctx", "ccT"):
                    hm[k] = shared[k]
        for _ in range(cores_per_batch):
            in_maps.append(hm)
    import time as _t
    _t0 = _t.time()
    res = run_bass_kernel_spmd(nc, in_maps, core_ids=list(range(len(in_maps))), **({"trace": True} if trace else {}))
    print("[kernel] device run (compile+transfer+exec) %.1fs" % (_t.time() - _t0), flush=True)
    return kb, res


def kernel(**inputs):
    inp = {k: np.asarray(v) for k, v in inputs.items()}
    B = inp["x"].shape[0]
    S = inp["x"].shape[1]
    L = inp["w_ada"].shape[0]
    kb, res = run(inp, S, L, B, cores_per_batch=1)
    out = np.stack([res.results[b]["out"] for b in range(B)], axis=0)
    return out.astype(np.float32)
```
